# Optimizing a Trainium2 kernel written in Bass

```python
import math
import jax, jax.numpy as jnp
from jax import lax
import numpy as np

D_MODEL = 2048
BATCH = 4
SEQ = 2048
DEPTH = 4
DEC_BATCH = 128
DEC_SEQ = 8
PAST_LEN = 16384
PAGE_SIZE = 128

N_EVEN = (DEPTH + 1) // 2
N_ODD = DEPTH // 2
D_POOL = D_MODEL // 2
POOL_WINDOWS = (2, 4, 8, 16)
N_POOL_GROUPS = len(POOL_WINDOWS)
POOL_GROUP = D_POOL // N_POOL_GROUPS
POOL_BUF = max(POOL_WINDOWS) - 1
D_RNN = D_MODEL // 2
RNN_HEADS = 8
RNN_HEAD_DIM = D_RNN // RNN_HEADS
CONV_W = 4
RG_C = 8.0
D_IN_EVEN = D_POOL + 2 * D_RNN
SSM_GROUP = 16
SSM_GROUPS = D_MODEL // SSM_GROUP
SSM_STATE = 64
SCAN_BLOCK = 128
D_FF = ((8 * D_MODEL // 3 + 127) // 128) * 128
RMS_EPS = 1e-6

kernel_name = "hybrid_pool_rglru_s5_macaron_step"


def rmsnorm(x, g):
    xf = x.astype(jnp.float32)
    y = xf * lax.rsqrt(jnp.mean(xf * xf, axis=-1, keepdims=True) + RMS_EPS)
    return (y * g.astype(jnp.float32)).astype(x.dtype)


def swiglu(x, w_in, w_out):
    gu = x @ w_in
    return (jax.nn.silu(gu[..., :D_FF]) * gu[..., D_FF:]) @ w_out


def pool_mixer(u, buf, pos0, w_grp, scale):
    bsz, L, _ = u.shape
    full = jnp.concatenate([buf, u], axis=1).astype(jnp.float32)
    cs = jnp.pad(jnp.cumsum(full, axis=1), ((0, 0), (1, 0), (0, 0)))
    pos = pos0 + jnp.arange(L)
    end = cs[:, POOL_BUF + 1:]
    means = []
    for g, w in enumerate(POOL_WINDOWS):
        sl = slice(g * POOL_GROUP, (g + 1) * POOL_GROUP)
        start = cs[:, POOL_BUF + 1 - w:POOL_BUF + 1 - w + L, sl]
        cnt = jnp.minimum(pos + 1, w).astype(jnp.float32)[None, :, None]
        means.append((end[..., sl] - start) / cnt)
    diff = (jnp.concatenate(means, axis=-1) - full[:, POOL_BUF:]).astype(u.dtype)
    diff = diff.reshape(bsz, L, N_POOL_GROUPS, POOL_GROUP)
    out = jnp.einsum('blgc,gcd->blgd', diff, w_grp).reshape(bsz, L, D_POOL) * scale
    return out, full[:, -POOL_BUF:].astype(u.dtype)


def causal_depthwise_conv(v, buf, w, b):
    full = jnp.concatenate([buf, v], axis=1)
    y = lax.conv_general_dilated(full, w[:, None, :], window_strides=(1,), padding='VALID',
                                 dimension_numbers=('NWC', 'WIO', 'NWC'),
                                 feature_group_count=v.shape[-1])
    return y + b, full[:, -(CONV_W - 1):]


def block_diag_linear(x, w, b):
    bsz, L, _ = x.shape
    xh = x.reshape(bsz, L, RNN_HEADS, RNN_HEAD_DIM)
    return jnp.einsum('blhi,hij->blhj', xh, w).reshape(bsz, L, D_RNN) + b


def rglru(x, h0, pos0, w_a, b_a, w_x, b_x, lam):
    L = x.shape[1]
    f32 = jnp.float32
    r = jax.nn.sigmoid(block_diag_linear(x, w_a, b_a).astype(f32))
    i = jax.nn.sigmoid(block_diag_linear(x, w_x, b_x).astype(f32))
    log_a = RG_C * r * jax.nn.log_sigmoid(lam.astype(f32))
    reset = (pos0 + jnp.arange(L) == 0)[None, :, None]
    a = jnp.where(reset, 0.0, jnp.exp(log_a))
    mult = jnp.where(reset, 1.0, jnp.sqrt(-jnp.expm1(2.0 * log_a)))
    b = mult * i * x.astype(f32)
    b = b.at[:, 0].add(a[:, 0] * h0.astype(f32))

    def combine(l, rr):
        return (l[0] * rr[0], rr[0] * l[1] + rr[1])

    _, h = lax.associative_scan(combine, (a, b), axis=1)
    return h, h[:, -1]


def even_mixer(hn, pool_buf, conv_buf, h0, pos0, w_in, pool_w, pool_scale, conv_w, conv_b,
               gate_a_w, gate_a_b, gate_x_w, gate_x_b, lam, w_out):
    z = hn @ w_in
    u_pool = z[..., :D_POOL]
    u_rnn = z[..., D_POOL:D_POOL + D_RNN]
    u_gate = z[..., D_POOL + D_RNN:]
    pool_out, new_pool = pool_mixer(u_pool, pool_buf, pos0, pool_w, pool_scale)
    conv_out, new_conv = causal_depthwise_conv(u_rnn, conv_buf, conv_w, conv_b)
    rec, new_h = rglru(conv_out, h0, pos0, gate_a_w, gate_a_b, gate_x_w, gate_x_b, lam)
    rnn_out = rec.astype(hn.dtype) * jax.nn.gelu(u_gate)
    y = jnp.concatenate([pool_out, rnn_out], axis=-1) @ w_out
    return y, new_pool, new_conv, new_h.astype(hn.dtype)


def complex_combine(l, r):
    a1r, a1i, b1r, b1i = l
    a2r, a2i, b2r, b2i = r
    return (a2r * a1r - a2i * a1i, a2r * a1i + a2i * a1r,
            a2r * b1r - a2i * b1i + b2r, a2r * b1i + a2i * b1r + b2i)


def s5_mixer(u, s_re, s_im, lam_re, lam_im, log_step, b_re, b_im, c_re, c_im, d, w_glu):
    bsz, L, _ = u.shape
    f32 = jnp.float32
    lr = jnp.minimum(lam_re.astype(f32), -1e-4)
    li = lam_im.astype(f32)
    dt = jnp.exp(log_step.astype(f32))[:, None]
    mag = jnp.exp(lr * dt)
    ab_re = mag * jnp.cos(li * dt)
    ab_im = mag * jnp.sin(li * dt)
    den = lr * lr + li * li
    nr = ab_re - 1.0
    f_re = (nr * lr + ab_im * li) / den
    f_im = (ab_im * lr - nr * li) / den
    br = b_re.astype(f32)
    bi = b_im.astype(f32)
    bb_re = f_re[..., None] * br - f_im[..., None] * bi
    bb_im = f_re[..., None] * bi + f_im[..., None] * br
    cr = c_re.astype(f32)
    ci = c_im.astype(f32)
    dg = d.astype(f32).reshape(SSM_GROUPS, SSM_GROUP)
    blk = math.gcd(L, SCAN_BLOCK)
    n_blk = L // blk
    ub = u.astype(f32).reshape(bsz, n_blk, blk, SSM_GROUPS, SSM_GROUP).transpose(1, 2, 0, 3, 4)
    a_re = jnp.broadcast_to(ab_re, (blk, bsz, SSM_GROUPS, SSM_STATE))
    a_im = jnp.broadcast_to(ab_im, (blk, bsz, SSM_GROUPS, SSM_STATE))

    def step(carry, ublk):
        xr0, xi0 = carry
        bu_re = jnp.einsum('tbgc,gpc->tbgp', ublk, bb_re)
        bu_im = jnp.einsum('tbgc,gpc->tbgp', ublk, bb_im)
        bu_re = bu_re.at[0].add(ab_re * xr0 - ab_im * xi0)
        bu_im = bu_im.at[0].add(ab_re * xi0 + ab_im * xr0)
        _, _, xr, xi = lax.associative_scan(complex_combine, (a_re, a_im, bu_re, bu_im), axis=0)
        y = (jnp.einsum('tbgp,gcp->tbgc', xr, cr) - jnp.einsum('tbgp,gcp->tbgc', xi, ci)
             + dg * ublk)
        return (xr[-1], xi[-1]), y

    (sr, si), yb = lax.scan(step, (s_re.astype(f32), s_im.astype(f32)), ub)
    y = yb.transpose(2, 0, 1, 3, 4).reshape(bsz, L, D_MODEL)
    z = jax.nn.gelu(y).astype(u.dtype) @ w_glu
    out = z[..., :D_MODEL] * jax.nn.sigmoid(z[..., D_MODEL:])
    return out, sr.astype(u.dtype), si.astype(u.dtype)


def trunk(x, st_pool, st_conv, st_h, st_re, st_im, pos0, p):
    new_pool, new_conv, new_h, new_re, new_im = [], [], [], [], []
    for layer in range(DEPTH):
        x = x + 0.5 * swiglu(rmsnorm(x, p['norm_ffn1'][layer]), p['w_ffn1_in'][layer], p['w_ffn1_out'][layer])
        hn = rmsnorm(x, p['norm_mix'][layer])
        if layer % 2 == 0:
            e = layer // 2
            y, pb, cb, hh = even_mixer(hn, st_pool[e], st_conv[e], st_h[e], pos0,
                                       p['w_in_even'][e], p['pool_w'][e], p['pool_scale'][e],
                                       p['conv_w'][e], p['conv_b'][e], p['gate_a_w'][e], p['gate_a_b'][e],
                                       p['gate_x_w'][e], p['gate_x_b'][e], p['rglru_lambda'][e],
                                       p['w_out_even'][e])
            new_pool.append(pb)
            new_conv.append(cb)
            new_h.append(hh)
        else:
            o = layer // 2
            y, sr, si = s5_mixer(hn, st_re[o], st_im[o], p['ssm_lambda_re'][o], p['ssm_lambda_im'][o],
                                 p['ssm_log_step'][o], p['ssm_b_re'][o], p['ssm_b_im'][o],
                                 p['ssm_c_re'][o], p['ssm_c_im'][o], p['ssm_d'][o], p['w_glu'][o])
            new_re.append(sr)
            new_im.append(si)
        x = x + y
        x = x + 0.5 * swiglu(rmsnorm(x, p['norm_ffn2'][layer]), p['w_ffn2_in'][layer], p['w_ffn2_out'][layer])
    return (rmsnorm(x, p['final_norm']), jnp.stack(new_pool), jnp.stack(new_conv), jnp.stack(new_h),
            jnp.stack(new_re), jnp.stack(new_im))


def setup_inputs(seed: int = 0) -> dict:
    key = jax.random.key(seed)
    ks = iter(jax.random.split(key, 40))
    f32 = jnp.float32

    def nrm(shape, s):
        return jax.random.normal(next(ks), shape, f32) * s

    rad = jnp.sqrt(jax.random.uniform(next(ks), (N_EVEN, D_RNN), f32, 0.81, 0.998))
    lam_im0 = jnp.broadcast_to(math.pi * jnp.arange(SSM_STATE, dtype=f32), (N_ODD, SSM_GROUPS, SSM_STATE))
    return {
        "x_prompt": nrm((BATCH, SEQ, D_MODEL), 1.0),
        "x_sample": nrm((DEC_BATCH, DEC_SEQ, D_MODEL), 1.0),
        "state_pool": nrm((N_EVEN, DEC_BATCH, POOL_BUF, D_POOL), 1.0),
        "state_conv": nrm((N_EVEN, DEC_BATCH, CONV_W - 1, D_RNN), 1.0),
        "state_rglru": nrm((N_EVEN, DEC_BATCH, D_RNN), 0.5),
        "state_ssm_re": nrm((N_ODD, DEC_BATCH, SSM_GROUPS, SSM_STATE), 0.1),
        "state_ssm_im": nrm((N_ODD, DEC_BATCH, SSM_GROUPS, SSM_STATE), 0.1),
        "norm_ffn1": 1.0 + nrm((DEPTH, D_MODEL), 0.02),
        "w_ffn1_in": nrm((DEPTH, D_MODEL, 2 * D_FF), D_MODEL ** -0.5),
        "w_ffn1_out": nrm((DEPTH, D_FF, D_MODEL), D_FF ** -0.5),
        "norm_mix": 1.0 + nrm((DEPTH, D_MODEL), 0.02),
        "norm_ffn2": 1.0 + nrm((DEPTH, D_MODEL), 0.02),
        "w_ffn2_in": nrm((DEPTH, D_MODEL, 2 * D_FF), D_MODEL ** -0.5),
        "w_ffn2_out": nrm((DEPTH, D_FF, D_MODEL), D_FF ** -0.5),
        "w_in_even": nrm((N_EVEN, D_MODEL, D_IN_EVEN), D_MODEL ** -0.5),
        "pool_w": nrm((N_EVEN, N_POOL_GROUPS, POOL_GROUP, POOL_GROUP), POOL_GROUP ** -0.5),
        "pool_scale": 1.0 + nrm((N_EVEN, D_POOL), 0.02),
        "conv_w": nrm((N_EVEN, CONV_W, D_RNN), CONV_W ** -0.5),
        "conv_b": nrm((N_EVEN, D_RNN), 0.01),
        "gate_a_w": nrm((N_EVEN, RNN_HEADS, RNN_HEAD_DIM, RNN_HEAD_DIM), RNN_HEAD_DIM ** -0.5),
        "gate_a_b": nrm((N_EVEN, D_RNN), 0.01),
        "gate_x_w": nrm((N_EVEN, RNN_HEADS, RNN_HEAD_DIM, RNN_HEAD_DIM), RNN_HEAD_DIM ** -0.5),
        "gate_x_b": nrm((N_EVEN, D_RNN), 0.01),
        "rglru_lambda": jnp.log(rad) - jnp.log1p(-rad),
        "w_out_even": nrm((N_EVEN, D_POOL + D_RNN, D_MODEL), (D_POOL + D_RNN) ** -0.5),
        "ssm_lambda_re": -0.5 + nrm((N_ODD, SSM_GROUPS, SSM_STATE), 0.01),
        "ssm_lambda_im": lam_im0 + nrm((N_ODD, SSM_GROUPS, SSM_STATE), 0.01),
        "ssm_log_step": jax.random.uniform(next(ks), (N_ODD, SSM_GROUPS), f32, math.log(1e-3), math.log(1e-1)),
        "ssm_b_re": nrm((N_ODD, SSM_GROUPS, SSM_STATE, SSM_GROUP), (2 * SSM_GROUP) ** -0.5),
        "ssm_b_im": nrm((N_ODD, SSM_GROUPS, SSM_STATE, SSM_GROUP), (2 * SSM_GROUP) ** -0.5),
        "ssm_c_re": nrm((N_ODD, SSM_GROUPS, SSM_GROUP, SSM_STATE), SSM_STATE ** -0.5),
        "ssm_c_im": nrm((N_ODD, SSM_GROUPS, SSM_GROUP, SSM_STATE), SSM_STATE ** -0.5),
        "ssm_d": nrm((N_ODD, D_MODEL), 1.0),
        "w_glu": nrm((N_ODD, D_MODEL, 2 * D_MODEL), D_MODEL ** -0.5),
        "final_norm": 1.0 + nrm((D_MODEL,), 0.02),
    }


def reference(x_prompt, x_sample, state_pool, state_conv, state_rglru, state_ssm_re, state_ssm_im,
              norm_ffn1, w_ffn1_in, w_ffn1_out, norm_mix, norm_ffn2, w_ffn2_in, w_ffn2_out,
              w_in_even, pool_w, pool_scale, conv_w, conv_b, gate_a_w, gate_a_b, gate_x_w, gate_x_b,
              rglru_lambda, w_out_even, ssm_lambda_re, ssm_lambda_im, ssm_log_step, ssm_b_re, ssm_b_im,
              ssm_c_re, ssm_c_im, ssm_d, w_glu, final_norm):
    p = dict(norm_ffn1=norm_ffn1, w_ffn1_in=w_ffn1_in, w_ffn1_out=w_ffn1_out, norm_mix=norm_mix,
             norm_ffn2=norm_ffn2, w_ffn2_in=w_ffn2_in, w_ffn2_out=w_ffn2_out, w_in_even=w_in_even,
             pool_w=pool_w, pool_scale=pool_scale, conv_w=conv_w, conv_b=conv_b, gate_a_w=gate_a_w,
             gate_a_b=gate_a_b, gate_x_w=gate_x_w, gate_x_b=gate_x_b, rglru_lambda=rglru_lambda,
             w_out_even=w_out_even, ssm_lambda_re=ssm_lambda_re, ssm_lambda_im=ssm_lambda_im,
             ssm_log_step=ssm_log_step, ssm_b_re=ssm_b_re, ssm_b_im=ssm_b_im, ssm_c_re=ssm_c_re,
             ssm_c_im=ssm_c_im, ssm_d=ssm_d, w_glu=w_glu, final_norm=final_norm)
    dt = x_prompt.dtype
    y_prompt, pool_p, conv_p, h_p, re_p, im_p = trunk(
        x_prompt,
        jnp.zeros((N_EVEN, BATCH, POOL_BUF, D_POOL), dt),
        jnp.zeros((N_EVEN, BATCH, CONV_W - 1, D_RNN), dt),
        jnp.zeros((N_EVEN, BATCH, D_RNN), dt),
        jnp.zeros((N_ODD, BATCH, SSM_GROUPS, SSM_STATE), dt),
        jnp.zeros((N_ODD, BATCH, SSM_GROUPS, SSM_STATE), dt),
        0, p)
    y_sample, pool_s, conv_s, h_s, re_s, im_s = trunk(
        x_sample, state_pool, state_conv, state_rglru, state_ssm_re, state_ssm_im, PAST_LEN, p)
    return (y_prompt, y_sample, pool_p, pool_s, conv_p, conv_s, h_p, h_s, re_p, re_s, im_p, im_s)
```

```python
import contextlib
import math
import numpy as np
import concourse.bass as bass
import concourse.mybir as mybir
from concourse.bass_utils import run_bass_kernel_spmd

F32 = mybir.dt.float32
BF16 = mybir.dt.bfloat16
I32 = mybir.dt.int32
AF = mybir.ActivationFunctionType
ALU = mybir.AluOpType

D = 2048
DFF = 5504
NFC = DFF // 128
DEPTH = 4
SEQ = 2048
NCORES = 8
NSAMP = 16
EPS = 1e-6
KMAX = 43
TWO_PI = 2.0 * math.pi

ENGS = ("pe", "act", "dve", "pool", "sp")

FULL_CFG = dict(blocks=[("p", 0), ("p", 1), ("p", 2), ("p", 3), ("s", 0)], layers=[0, 1, 2, 3],
                ffn=True, mixer=True, final=True, last_p=3)


class Prog:
    def __init__(self):
        self.ops = {e: [] for e in ENGS}
        self.cnt = {e: 0 for e in ENGS}
        self.waited = {e: {} for e in ENGS}
        self.buf = {}
        self.ndma = 24
        self.dma_val = {}
        self.dma_rr = {e: 0 for e in ENGS}
        self.out_tokens = []
        self.dma_live = []

    def _wait(self, e, tok, waits, force=False):
        kind, key, val = tok
        if kind == "eng" and key == e and not force:
            return
        k = (kind, key)
        if self.waited[e].get(k, 0) >= val:
            return
        self.waited[e][k] = val
        waits.append(tok)

    def emit(self, e, fn, reads=(), writes=(), dma=False, extra=(), hard=()):
        deps = list(extra)
        hard = list(hard)
        for k in reads:
            b = self.buf.get(k)
            if b and b[0]:
                (hard if e != "pe" else deps).append(b[0])
        for k in writes:
            b = self.buf.get(k)
            if b:
                tgt = hard if e != "pe" else deps
                if b[0]:
                    tgt.append(b[0])
                tgt.extend(b[1])
        waits = []
        for d in deps:
            self._wait(e, d, waits)
        for d in hard:
            self._wait(e, d, waits, force=True)
        if dma:
            idx = self.dma_rr[e]
            self.dma_rr[e] = (idx + 1) % self.ndma
            prev = self.dma_val.get((e, idx), 0)
            if prev:
                self._wait(e, ("dma", (e, idx), prev), waits)
            val = prev + 16
            self.dma_val[(e, idx)] = val
            tok = ("dma", (e, idx), val)
            self.dma_live.append(tok)
        elif fn == "nop":
            tok = ("eng", e, self.cnt[e])
        else:
            self.cnt[e] += 1
            tok = ("eng", e, self.cnt[e])
        self.ops[e].append((waits, fn, tok))
        for k in reads:
            self.buf.setdefault(k, [None, []])[1].append(tok)
        for k in writes:
            self.buf[k] = [tok, []]
        return tok

    def barrier(self):
        toks = [("eng", e, self.cnt[e]) for e in ENGS if self.cnt[e] > 0] + self.dma_live
        self.dma_live = []
        for e in ("pe", "act", "dve", "pool", "sp"):
            self.emit(e, "nop", extra=toks)


def build_program(cfg=None):
    cfg = cfg or FULL_CFG
    nc = bass.Bass("TRN2", target_bir_lowering=False)
    P = Prog()
    es = contextlib.ExitStack()
    in_names = []

    def dram(name, shape, kind="ExternalInput", dt=F32):
        if kind == "ExternalInput":
            in_names.append(name)
        return nc.dram_tensor(name, list(shape), dt, kind=kind).ap()

    layers = cfg["layers"]
    do_ffn, do_mix = cfg["ffn"], cfg["mixer"]
    evens = [l for l in layers if l % 2 == 0]
    odds = [l for l in layers if l % 2 == 1]

    xp = dram("xp", [SEQ, D])
    xs_in = dram("xs", [NSAMP * 8, D])
    norms = dram("norms", [13, D])
    yp = dram("yp", [SEQ, D], kind="ExternalOutput")
    ys = dram("ys", [NSAMP * 8, D], kind="ExternalOutput")
    if do_ffn:
        w1i = dram("w_ffn1_in", [DEPTH, D, 2 * DFF])
        w1o = dram("w_ffn1_out", [DEPTH, DFF, D])
        w2i = dram("w_ffn2_in", [DEPTH, D, 2 * DFF])
        w2o = dram("w_ffn2_out", [DEPTH, DFF, D])
    if do_mix and evens:
        w_in_even = dram("w_in_even", [2, D, 3072])
        pool_w = dram("pool_w", [2, 4, 256, 256])
        evec = dram("evec", [2, 9, 1024])
        gate_a_w = dram("gate_a_w", [2, 8, 128, 128])
        gate_x_w = dram("gate_x_w", [2, 8, 128, 128])
        w_out_even = dram("w_out_even", [2, D, D])
        st_pool = dram("st_pool", [2, NSAMP * 15, 1024])
        st_conv = dram("st_conv", [2, NSAMP * 3, 1024])
        st_h = dram("st_h", [2, NSAMP, 1024])
    o_pool_p = dram("o_pool_p", [2, 15, 1024], kind="ExternalOutput")
    o_pool_s = dram("o_pool_s", [2, NSAMP * 15, 1024], kind="ExternalOutput")
    o_conv_p = dram("o_conv_p", [2, 3, 1024], kind="ExternalOutput")
    o_conv_s = dram("o_conv_s", [2, NSAMP * 3, 1024], kind="ExternalOutput")
    o_h_p = dram("o_h_p", [2, 1, 1024], kind="ExternalOutput")
    o_h_s = dram("o_h_s", [2, NSAMP, 1024], kind="ExternalOutput")
    o_re_p = dram("o_re_p", [2, 128, 64], kind="ExternalOutput")
    o_re_s = dram("o_re_s", [2, NSAMP, 128, 64], kind="ExternalOutput")
    o_im_p = dram("o_im_p", [2, 128, 64], kind="ExternalOutput")
    o_im_s = dram("o_im_s", [2, NSAMP, 128, 64], kind="ExternalOutput")
    if do_mix and odds:
        lam_re = dram("ssm_lambda_re", [2, 128, 64])
        lam_im = dram("ssm_lambda_im", [2, 128, 64])
        log_step = dram("ssm_log_step", [2, 128])
        b_re = dram("ssm_b_re", [2, 128, 64, 16])
        b_im = dram("ssm_b_im", [2, 128, 64, 16])
        c_re = dram("ssm_c_re", [2, 128 * 16, 64])
        c_im = dram("ssm_c_im", [2, 128 * 16, 64])
        ssm_d = dram("ssm_d", [2, D])
        w_glu = dram("w_glu", [2, D, 2 * D])
        st_re = dram("st_re", [2, NSAMP, 128, 64])
        st_im = dram("st_im", [2, NSAMP, 128, 64])
        M_all = [dram(f"M_all{o}", [128, 128, 128], kind="Internal", dt=BF16) for o in range(2)]
        WT_all = [dram(f"WT_all{o}", [128, 128, 128], kind="Internal", dt=BF16) for o in range(2)]
        CA_all = [dram(f"CA_all{o}", [64, 128, 256], kind="Internal", dt=BF16) for o in range(2)]

    def sb(name, shape, dt=F32):
        return es.enter_context(nc.sbuf_tensor(name, list(shape), dt))

    def ps(name, shape, dt=F32):
        return es.enter_context(nc.psum_tensor(name, list(shape), dt))

    x = sb("x", [128, 4, D])
    xsc = sb("xsc", [128, D], BF16)
    xnT = sb("xnT", [128, 16, 512], BF16)
    ARENA_F = 21504
    arena = sb("arena", [128, ARENA_F])
    NW = 3
    wsl = [sb(f"w{i}", [128, KMAX, 128], BF16) for i in range(NW)]
    gcol = sb("gcol", [128, 13, 16])
    ident_b = sb("ident_b", [128, 128], BF16)
    ident_f = sb("ident_f", [128, 128])
    stat = sb("stat", [128, 8])
    tmpa = sb("tmpa", [128, 512])
    tmpb = sb("tmpb", [128, 512])
    NPB = 4
    pbank = [ps(f"pb{i}", [128, 512]) for i in range(NPB)]
    ptr_b = ps("ptr_b", [128, 4, 128], BF16)
    ptr_f = [ps(f"ptr_f{i}", [128, 4, 128]) for i in range(2)]

    def carve(off, words, shape=None, dt=F32):
        v = arena[:, off:off + words]
        if dt == BF16:
            v = v.bitcast(BF16)
        return v

    hT = carve(0, NFC * 256, dt=BF16).rearrange("p (c n) -> p c n", n=512)

    sems = {e: es.enter_context(nc.semaphore(f"s_{e}")) for e in ENGS}
    dsems = {(e, i): es.enter_context(nc.semaphore(f"d_{e}{i}")) for e in ("sp", "pool") for i in range(P.ndma)}

    st = {"w": 0, "pb": 0, "pf": 0}

    def dma(q, out, in_, reads=(), writes=(), **kw):
        return P.emit(q, lambda e: e.dma_start(out=out, in_=in_, **kw), reads=reads, writes=writes, dma=True)

    def op(e, f, reads=(), writes=(), chain=False, **kw):
        if chain and P.cnt[e] > 0:
            kw["hard"] = list(kw.get("hard", ())) + [("eng", e, P.cnt[e])]
        return P.emit(e, f, reads=reads, writes=writes, **kw)

    def load_w(w2d, K, col0, ncols=128):
        i = st["w"]
        st["w"] = (i + 1) % NW
        assert K * ncols <= KMAX * 128
        src = w2d[:, col0:col0 + ncols].rearrange("(k p) c -> p k c", p=128)
        view = wsl[i][:].rearrange("p k c -> p (k c)")[:, 0:K * ncols].rearrange("p (k c) -> p k c", c=ncols)
        dma("pool", view, src, writes=[("w", i)])
        return (i, view)

    def next_bank():
        b = st["pb"]
        st["pb"] = (b + 1) % NPB
        return b

    def proj(wh, K, rhs_fn, NB, rhs_keys, sub=0):
        wi, view = wh
        b = next_bank()
        for k in range(K):
            op("pe", (lambda k: lambda e: e.matmul(pbank[b][:, 0:NB], lhsT=view[:, k, sub * 128:(sub + 1) * 128], rhs=rhs_fn(k),
                                                  start=(k == 0), stop=(k == K - 1)))(k),
               reads=[("w", wi)] + list(rhs_keys), writes=[("pb", b)])
        return b

    def rstd_of_tile(i):
        op("act", lambda e: e.activation(out=xsc[:], in_=x[:, i, :], func=AF.Square, accum_out=stat[:, 0:1]),
           reads=[("x", i)], writes=["xsc", ("stat", 0)])
        op("act", lambda e: e.activation(out=stat[:, 1:2], in_=stat[:, 0:1], func=AF.Sqrt,
                                         scale=1.0 / D, bias=stat[:, 3:4]),
           reads=[("stat", 0), "epsb"], writes=[("stat", 1)], hard=[P.buf[("stat", 0)][0]])
        op("dve", lambda e: e.reciprocal(out=stat[:, 2:3], in_=stat[:, 1:2]),
           reads=[("stat", 1)], writes=[("stat", 2)])

    def rms_to_T(TB, gi):
        for i in range(TB):
            rstd_of_tile(i)
            op("act", lambda e, i=i: e.activation(out=xsc[:], in_=x[:, i, :], func=AF.Copy, scale=stat[:, 2:3]),
               reads=[("x", i), ("stat", 2)], writes=["xsc"])
            for q in range(4):
                for j in range(4):
                    kc = q * 4 + j
                    op("pe", lambda e, kc=kc, j=j: e.transpose(out=ptr_b[:, j, :], in_=xsc[:, kc * 128:(kc + 1) * 128],
                                                             identity=ident_b[:]),
                       reads=["xsc", "identb"], writes=["ptr_b"])
                op("dve", lambda e, q=q, i=i: e.tensor_tensor(
                    out=xnT[:, q * 4:q * 4 + 4, i * 128:(i + 1) * 128], in0=ptr_b[:],
                    in1=gcol[:, gi, q * 4:q * 4 + 4].unsqueeze(2).broadcast_to([128, 4, 128]), op=ALU.mult),
                   reads=["ptr_b", "gcol"], writes=["xnT"])

    def add_T_into_x(src, skey, TB, dcol):
        f = st["pf"]
        st["pf"] = 1 - f
        for i in range(TB):
            op("pe", lambda e, i=i: e.transpose(out=ptr_f[f][:, i, :], in_=src[:, i * 128:(i + 1) * 128],
                                               identity=ident_f[:]),
               reads=[skey, "identf"], writes=[("ptr_f", f)])
        op("dve", lambda e: e.tensor_tensor(out=x[:, 0:TB, dcol * 128:(dcol + 1) * 128],
                                           in0=ptr_f[f][:, 0:TB, :],
                                           in1=x[:, 0:TB, dcol * 128:(dcol + 1) * 128], op=ALU.add),
           reads=[("ptr_f", f)], writes=[("x", i) for i in range(TB)])

    def rows_out(src_fn, nrows, ntiles, dst, skeys):
        rowbuf = tmpa if nrows <= 128 else None
        rb = arena[:, ARENA_F - 1024:ARENA_F]
        for ct in range(ntiles):
            f = st["pf"]
            st["pf"] = 1 - f
            pview = ptr_f[f][:].rearrange("p a b -> p (a b)")
            op("pe", lambda e, ct=ct, pview=pview: e.transpose(out=pview[0:nrows, 0:128], in_=src_fn(ct),
                                                             identity=ident_f[:]),
               reads=list(skeys) + ["identf"], writes=[("ptr_f", f)])
            op("act", lambda e, ct=ct, pview=pview: e.activation(out=rb[0:nrows, ct * 128:(ct + 1) * 128],
                                                               in_=pview[0:nrows, 0:128], func=AF.Copy),
               reads=[("ptr_f", f)], writes=["rowbuf"])
        t = dma("sp", dst, rb[0:nrows, 0:ntiles * 128], reads=["rowbuf"])
        P.out_tokens.append(t)

    def rows_in(src, nrows, ntiles, dst_fn, dkeys):
        rb = arena[:, ARENA_F - 1024:ARENA_F]
        dma("sp", rb[0:nrows, 0:ntiles * 128], src, writes=["rowbuf"])
        for ct in range(ntiles):
            f = st["pf"]
            st["pf"] = 1 - f
            pview = ptr_f[f][:].rearrange("p a b -> p (a b)")
            op("pe", lambda e, ct=ct, pview=pview: e.transpose(out=pview[:, 0:nrows],
                                                             in_=rb[0:nrows, ct * 128:(ct + 1) * 128],
                                                             identity=ident_f[0:nrows, 0:nrows]),
               reads=["rowbuf", "identf"], writes=[("ptr_f", f)])
            op("act", lambda e, ct=ct, pview=pview: e.activation(out=dst_fn(ct), in_=pview[:, 0:nrows], func=AF.Copy),
               reads=[("ptr_f", f)], writes=list(dkeys))

    def gelu_tanh(src, skeys, out, okeys, NB, mul_in=None, mkeys=()):
        t1 = tmpa[:, 0:NB]
        t2 = tmpb[:, 0:NB]
        op("act", lambda e: e.activation(out=t1, in_=src, func=AF.Square), reads=skeys, writes=["tmpa"])
        op("dve", lambda e: e.tensor_scalar(out=t1, in0=t1, scalar1=0.044715, scalar2=1.0, op0=ALU.mult, op1=ALU.add),
           reads=["tmpa"], writes=["tmpa"])
        op("dve", lambda e: e.tensor_tensor(out=t1, in0=src, in1=t1, op=ALU.mult), reads=list(skeys) + ["tmpa"],
           writes=["tmpa"])
        op("act", lambda e: e.activation(out=t2, in_=t1, func=AF.Sigmoid, scale=1.5957691216057308),
           reads=["tmpa"], writes=["tmpb"])
        if mul_in is None:
            op("dve", lambda e: e.tensor_tensor(out=out, in0=src, in1=t2, op=ALU.mult),
               reads=list(skeys) + ["tmpb"], writes=okeys)
        else:
            op("dve", lambda e: e.tensor_tensor(out=t2, in0=src, in1=t2, op=ALU.mult),
               reads=list(skeys) + ["tmpb"], writes=["tmpb"])
            op("dve", lambda e: e.tensor_tensor(out=out, in0=t2, in1=mul_in, op=ALU.mult),
               reads=["tmpb"] + list(mkeys), writes=okeys)

    def ffn(TB, gi, w_in, w_out):
        NB = TB * 128
        rms_to_T(TB, gi)
        xr = lambda k: xnT[:, k, 0:NB]
        for c0 in range(0, NFC, 2):
            n_ = min(2, NFC - c0)
            wg = load_w(w_in, 16, c0 * 128, ncols=n_ * 128)
            wu = load_w(w_in, 16, DFF + c0 * 128, ncols=n_ * 128)
            for j in range(n_):
                c = c0 + j
                bg = proj(wg, 16, xr, NB, ["xnT"], sub=j)
                bu = proj(wu, 16, xr, NB, ["xnT"], sub=j)
                op("act", lambda e, bg=bg: e.activation(out=tmpa[:, 0:NB], in_=pbank[bg][:, 0:NB], func=AF.Silu),
                   reads=[("pb", bg)], writes=["tmpa"])
                op("dve", lambda e, bu=bu, c=c: e.tensor_tensor(out=hT[:, c, 0:NB], in0=pbank[bu][:, 0:NB],
                                                               in1=tmpa[:, 0:NB], op=ALU.mult),
                   reads=[("pb", bu), "tmpa"], writes=["hT"])
        for dc0 in range(0, 16, 2):
            banks = (next_bank(), next_bank())
            kb = 0
            for K_ in (21, 21, 1):
                wi, view = load_w(w_out[kb * 128:(kb + K_) * 128, :], K_, dc0 * 128, ncols=256)
                for k in range(K_):
                    for j, b in enumerate(banks):
                        op("pe", lambda e, k=k, j=j, b=b, kb=kb, view=view: e.matmul(
                            pbank[b][:, 0:NB], lhsT=view[:, k, j * 128:(j + 1) * 128], rhs=hT[:, kb + k, 0:NB],
                            start=(kb + k == 0), stop=(kb + k == NFC - 1)),
                           reads=[("w", wi), "hT"], writes=[("pb", b)])
                kb += K_
            for j, b in enumerate(banks):
                op("act", lambda e, b=b: e.activation(out=tmpb[:, 0:NB], in_=pbank[b][:, 0:NB], func=AF.Copy, scale=0.5),
                   reads=[("pb", b)], writes=["tmpb"])
                add_T_into_x(tmpb, "tmpb", TB, dc0 + j)

    def final_norm_store(TB, out_ap, do_norm):
        gfin = arena[:, 0:D]
        yout = arena[:, D:2 * D]
        if do_norm:
            dma("sp", gfin, norms[12:13, :].broadcast_to([128, D]), writes=["gfin"])
        for i in range(TB):
            if do_norm:
                rstd_of_tile(i)
                op("dve", lambda e, i=i: e.scalar_tensor_tensor(out=yout, in0=x[:, i, :], scalar=stat[:, 2:3],
                                                               in1=gfin, op0=ALU.mult, op1=ALU.mult),
                   reads=[("x", i), ("stat", 2), "gfin"], writes=["yout"])
                t = dma("sp", out_ap[i * 128:(i + 1) * 128, :], yout, reads=["yout"])
            else:
                t = dma("sp", out_ap[i * 128:(i + 1) * 128, :], x[:, i, :], reads=[("x", i)])
            P.out_tokens.append(t)

    if do_mix and evens:
        evc = sb("evc", [128, 2, 9, 8])
        clam = sb("clam", [128, 2, 8])
        cfix = sb("cfix", [128, 4, 15])
        ones_c = sb("ones_c", [128, 1])
        poolbuf_p = sb("poolbuf_p", [128, 2, 8, 15])
        convbuf_p = sb("convbuf_p", [128, 2, 8, 3])
        hst_p = sb("hst_p", [128, 2, 8])
        poolbuf_s = x[:].rearrange("p a b -> p (a b)")[:, 6144:6144 + 8 * NSAMP * 15].rearrange("p (c b r) -> p c b r", b=NSAMP, r=15)
        convbuf_s = sb("convbuf_s", [128, 8, NSAMP, 3])
        hst_s = sb("hst_s", [128, 8, NSAMP])

    def even_setup():
        for e in range(2):
            for v in range(9):
                dma("sp", evc[:, e, v, :], evec[e, v].rearrange("(k p) -> p k", p=128), writes=["evc"],
                    allow_slow_non_contiguous=True)
        op("act", lambda e: e.activation(out=clam[:], in_=evc[:, :, 8, :], func=AF.Exp, scale=-1.0),
           reads=["evc"], writes=["clam"])
        op("act", lambda e: e.activation(out=clam[:], in_=clam[:], func=AF.Ln, bias=ones_c[:, 0:1]),
           reads=["clam", "ones"], writes=["clam"])
        op("dve", lambda e: e.tensor_scalar(out=clam[:], in0=clam[:], scalar1=-8.0, scalar2=None, op0=ALU.mult),
           reads=["clam"], writes=["clam"])
        for g in range(4):
            w = float(2 << g)
            for pos in range(15):
                op("dve", lambda e, g=g, pos=pos, w=w: e.memset(cfix[:, g, pos:pos + 1], w / min(pos + 1.0, w)),
                   writes=["cfix"])
        op("dve", lambda e: e.memset(poolbuf_p[:], 0.0), writes=["poolbuf_p"])
        op("dve", lambda e: e.memset(convbuf_p[:], 0.0), writes=["convbuf_p"])
        op("dve", lambda e: e.memset(hst_p[:], 0.0), writes=["hst_p"])

    def even_mixer(kind, bidx, l):
        e = l // 2
        samp = kind == "s"
        TB = 1 if samp else 4
        NB = TB * 128
        nseg, L = (NSAMP, 8) if samp else (1, 512)
        first = (not samp) and bidx == 0
        last = (not samp) and bidx == cfg["last_p"]
        catT = carve(0, 16 * 256, dt=BF16).rearrange("p (c n) -> p c n", n=512)
        off = 16 * 256
        def fbuf(words):
            nonlocal off
            v = arena[:, off:off + words]
            off += words
            return v
        FW = 544
        Fb, sA, sB = fbuf(FW), fbuf(FW), fbuf(FW)
        diffT = carve(off, 512, dt=BF16).rearrange("p (c n) -> p c n", n=512); off += 512
        cv, rg, ig, ag, mg, hs = (fbuf(512) for _ in range(6))
        cvb = carve(off, 256, dt=BF16); off += 256

        def seg(v, width):
            return v[:, 0:nseg * width].rearrange("p (s w) -> p s w", w=width)

        if samp:
            pbuf = lambda ct: poolbuf_s[:, ct, :, :]
            cbuf = lambda h: convbuf_s[:, h, :, :]
            hst = lambda h: hst_s[:, h, :]
            pk, ck, hk = "poolbuf_s", "convbuf_s", "hst_s"
            rows_in(st_pool[e, 0:120, :], 120, 8,
                    lambda ct: poolbuf_s[:, ct, 0:8, :].rearrange("p b r -> p (b r)"), [pk])
            rows_in(st_pool[e, 120:240, :], 120, 8,
                    lambda ct: poolbuf_s[:, ct, 8:16, :].rearrange("p b r -> p (b r)"), [pk])
            rows_in(st_conv[e], 48, 8, lambda ct: convbuf_s[:, ct, :, :].rearrange("p b r -> p (b r)"), [ck])
            rows_in(st_h[e], 16, 8, lambda ct: hst_s[:, ct, :], [hk])
        else:
            pbuf = lambda ct: poolbuf_p[:, e, ct, :].unsqueeze(1)
            cbuf = lambda h: convbuf_p[:, e, h, :].unsqueeze(1)
            hst = lambda h: hst_p[:, e, h:h + 1]
            pk, ck, hk = "poolbuf_p", "convbuf_p", "hst_p"

        rms_to_T(TB, 4 + l)
        zrhs = lambda k: xnT[:, k, 0:NB]
        W15 = 15 + L
        F3 = seg(Fb, W15)
        for g in range(4):
            w = 2 << g
            for j2 in range(2):
                ct = 2 * g + j2
                wi = load_w(w_in_even[e], 16, ct * 128)
                bz = proj(wi, 16, zrhs, NB, ["xnT"])
                op("act", lambda e_, ct=ct: e_.activation(out=F3[:, :, 0:15], in_=pbuf(ct), func=AF.Copy),
                   reads=[pk], writes=["Fb"])
                op("act", lambda e_, bz=bz: e_.activation(out=F3[:, :, 15:W15],
                                                         in_=pbank[bz][:, 0:NB].rearrange("p (s w) -> p s w", w=L),
                                                         func=AF.Copy),
                   reads=[("pb", bz)], writes=["Fb"])
                cur, ckey, m = F3, "Fb", 1
                for lev in range(g + 1):
                    nx = seg(sA if lev % 2 == 0 else sB, W15)
                    nkey = "sA" if lev % 2 == 0 else "sB"
                    op("dve", lambda e_, cur=cur, nx=nx, m=m: e_.tensor_tensor(
                        out=nx[:, :, 2 * m - 1:W15], in0=cur[:, :, 2 * m - 1:W15], in1=cur[:, :, m - 1:W15 - m],
                        op=ALU.add), reads=[ckey], writes=[nkey])
                    cur, ckey, m = nx, nkey, 2 * m
                if first:
                    op("dve", lambda e_, cur=cur, g=g: e_.tensor_tensor(out=cur[:, 0, 15:30], in0=cur[:, 0, 15:30],
                                                                       in1=cfix[:, g, :], op=ALU.mult),
                       reads=[ckey, "cfix"], writes=[ckey])
                op("dve", lambda e_, cur=cur, j2=j2, w=w: e_.scalar_tensor_tensor(
                    out=diffT[:, j2, 0:NB].rearrange("p (s w) -> p s w", w=L), in0=cur[:, :, 15:W15],
                    scalar=1.0 / w, in1=F3[:, :, 15:W15], op0=ALU.mult, op1=ALU.subtract),
                   reads=[ckey, "Fb"], writes=["diffT"])
                op("act", lambda e_, ct=ct: e_.activation(out=pbuf(ct), in_=F3[:, :, L:L + 15], func=AF.Copy),
                   reads=["Fb"], writes=[pk])
            for j in range(2):
                wi = load_w(pool_w[e, g], 2, j * 128)
                bp = proj(wi, 2, lambda k: diffT[:, k, 0:NB], NB, ["diffT"])
                op("act", lambda e_, bp=bp, g=g, j=j: e_.activation(out=catT[:, 2 * g + j, 0:NB], in_=pbank[bp][:, 0:NB],
                                                                   func=AF.Copy, scale=evc[:, e, 0, 2 * g + j:2 * g + j + 1]),
                   reads=[("pb", bp), "evc"], writes=["catT"])
        W3 = 3 + L
        C3 = seg(Fb, W3)
        cv3, r3, i3, a3, m3, h3 = (seg(v, L) for v in (cv, rg, ig, ag, mg, hs))
        for h in range(8):
            wi = load_w(w_in_even[e], 16, (8 + h) * 128)
            bz = proj(wi, 16, zrhs, NB, ["xnT"])
            op("act", lambda e_, h=h: e_.activation(out=C3[:, :, 0:3], in_=cbuf(h), func=AF.Copy),
               reads=[ck], writes=["Fb"])
            op("act", lambda e_, bz=bz: e_.activation(out=C3[:, :, 3:W3],
                                                     in_=pbank[bz][:, 0:NB].rearrange("p (s w) -> p s w", w=L),
                                                     func=AF.Copy),
               reads=[("pb", bz)], writes=["Fb"])
            op("dve", lambda e_, h=h: e_.tensor_scalar(out=cv3, in0=C3[:, :, 3:W3], scalar1=evc[:, e, 4, h:h + 1],
                                                      scalar2=evc[:, e, 5, h:h + 1], op0=ALU.mult, op1=ALU.add),
               reads=["Fb", "evc"], writes=["cv"])
            for k in (2, 1, 0):
                op("dve", lambda e_, h=h, k=k: e_.scalar_tensor_tensor(out=cv3, in0=C3[:, :, k:k + L],
                                                                      scalar=evc[:, e, 1 + k, h:h + 1], in1=cv3,
                                                                      op0=ALU.mult, op1=ALU.add),
                   reads=["Fb", "evc", "cv"], writes=["cv"])
            op("act", lambda e_, h=h: e_.activation(out=cbuf(h), in_=C3[:, :, L:L + 3], func=AF.Copy),
               reads=["Fb"], writes=[ck])
            op("act", lambda e_: e_.activation(out=cvb[:, 0:NB], in_=cv[:, 0:NB], func=AF.Copy),
               reads=["cv"], writes=["cvb"])
            wa = load_w(gate_a_w[e, h], 1, 0)
            ba = proj(wa, 1, lambda k: cvb[:, 0:NB], NB, ["cvb"])
            op("act", lambda e_, ba=ba, h=h: e_.activation(out=rg[:, 0:NB], in_=pbank[ba][:, 0:NB], func=AF.Sigmoid,
                                                          bias=evc[:, e, 6, h:h + 1]),
               reads=[("pb", ba), "evc"], writes=["rg"])
            wx = load_w(gate_x_w[e, h], 1, 0)
            bx = proj(wx, 1, lambda k: cvb[:, 0:NB], NB, ["cvb"])
            op("act", lambda e_, bx=bx, h=h: e_.activation(out=ig[:, 0:NB], in_=pbank[bx][:, 0:NB], func=AF.Sigmoid,
                                                          bias=evc[:, e, 7, h:h + 1]),
               reads=[("pb", bx), "evc"], writes=["ig"])
            op("act", lambda e_, h=h: e_.activation(out=ag[:, 0:NB], in_=rg[:, 0:NB], func=AF.Exp,
                                                   scale=clam[:, e, h:h + 1]),
               reads=["rg", "clam"], writes=["ag"])
            op("dve", lambda e_: e_.tensor_tensor(out=mg[:, 0:NB], in0=ag[:, 0:NB], in1=ag[:, 0:NB], op=ALU.mult),
               reads=["ag"], writes=["mg"])
            op("act", lambda e_: e_.activation(out=mg[:, 0:NB], in_=mg[:, 0:NB], func=AF.Sqrt, scale=-1.0,
                                              bias=ones_c[:, 0:1]),
               reads=["mg", "ones"], writes=["mg"])
            if first:
                op("dve", lambda e_: e_.memset(ag[:, 0:1], 0.0), reads=["ag"], writes=["ag"])
                op("dve", lambda e_: e_.memset(mg[:, 0:1], 1.0), reads=["mg"], writes=["mg"])
            op("dve", lambda e_: e_.tensor_tensor(out=ig[:, 0:NB], in0=ig[:, 0:NB], in1=mg[:, 0:NB], op=ALU.mult),
               reads=["ig", "mg"], writes=["ig"])
            op("dve", lambda e_: e_.tensor_tensor(out=ig[:, 0:NB], in0=ig[:, 0:NB], in1=cv[:, 0:NB], op=ALU.mult),
               reads=["ig", "cv"], writes=["ig"])
            for s in range(nseg):
                op("dve", lambda e_, s=s, h=h: e_.tensor_tensor_scan(out=h3[:, s, :], data0=a3[:, s, :], data1=i3[:, s, :],
                                                                    initial=hst(h)[:, s:s + 1], op0=ALU.mult, op1=ALU.add),
                   reads=["ag", "ig", hk], writes=["hs"])
            op("act", lambda e_, h=h: e_.activation(out=hst(h), in_=h3[:, :, L - 1], func=AF.Copy),
               reads=["hs"], writes=[hk])
            wg = load_w(w_in_even[e], 16, (16 + h) * 128)
            bg = proj(wg, 16, zrhs, NB, ["xnT"])
            gelu_tanh(pbank[bg][:, 0:NB], [("pb", bg)], catT[:, 8 + h, 0:NB], ["catT"], NB,
                      mul_in=hs[:, 0:NB], mkeys=["hs"])
        for dc in range(16):
            wo = load_w(w_out_even[e], 16, dc * 128)
            by = proj(wo, 16, lambda k: catT[:, k, 0:NB], NB, ["catT"])
            op("act", lambda e_, by=by: e_.activation(out=tmpb[:, 0:NB], in_=pbank[by][:, 0:NB], func=AF.Copy),
               reads=[("pb", by)], writes=["tmpb"])
            add_T_into_x(tmpb, "tmpb", TB, dc)
        if samp:
            rows_out(lambda ct: poolbuf_s[:, ct, 0:8, :].rearrange("p b r -> p (b r)"), 120, 8, o_pool_s[e, 0:120, :], [pk])
            rows_out(lambda ct: poolbuf_s[:, ct, 8:16, :].rearrange("p b r -> p (b r)"), 120, 8, o_pool_s[e, 120:240, :], [pk])
            rows_out(lambda ct: convbuf_s[:, ct, :, :].rearrange("p b r -> p (b r)"), 48, 8, o_conv_s[e], [ck])
            rows_out(lambda ct: hst_s[:, ct, :], 16, 8, o_h_s[e], [hk])
        elif last:
            rows_out(lambda ct: poolbuf_p[:, e, ct, :], 15, 8, o_pool_p[e], [pk])
            rows_out(lambda ct: convbuf_p[:, e, ct, :], 3, 8, o_conv_p[e], [ck])
            rows_out(lambda ct: hst_p[:, e, ct:ct + 1], 1, 8, o_h_p[e], [hk])

    if do_mix and odds:
        Fsel = sb("Fsel", [128, 8, 8, 128], BF16)
        mask_ts = sb("mask_ts", [128, 128])
        AB8 = sb("AB8", [64, 2, 2, 128])
        Xcar = sb("Xcar", [64, 2, 2, 128])
        dcol = sb("dcol", [128, 2, 16])
        Xs_in = x[:].rearrange("p a b -> p (a b)")[0:64, 2048:6144].rearrange("p (r g b) -> p r g b", g=128, b=NSAMP)
        Xs_out = Xs_in

    def sincos(turns, s_out, c_out, shape, keys):
        n = 1
        for v in shape[1:]:
            n *= v
        ti = tmpa[0:shape[0], 0:n].bitcast(I32)
        tf = tmpb[0:shape[0], 0:n]
        tm = tmpb[0:shape[0], n:2 * n]
        for (shift, dst) in ((0.0, s_out), (0.25, c_out)):
            flat = lambda ap: ap
            op("dve", lambda e, shift=shift: e.tensor_scalar(out=tf, in0=turns, scalar1=shift, scalar2=None, op0=ALU.add),
               reads=keys, writes=["tmpb"])
            op("dve", lambda e: e.tensor_copy(out=ti, in_=tf), reads=["tmpb"], writes=["tmpa"])
            op("dve", lambda e: e.tensor_copy(out=tm, in_=ti), reads=["tmpa"], writes=["tmpb"])
            op("dve", lambda e: e.tensor_tensor(out=tf, in0=tf, in1=tm, op=ALU.subtract), reads=["tmpb"], writes=["tmpb"])
            op("dve", lambda e: e.tensor_scalar(out=tm, in0=tf, scalar1=0.5, scalar2=None, op0=ALU.is_gt),
               reads=["tmpb"], writes=["tmpb"])
            op("dve", lambda e: e.tensor_tensor(out=tf, in0=tf, in1=tm, op=ALU.subtract), reads=["tmpb"], writes=["tmpb"])
            op("dve", lambda e: e.tensor_scalar(out=tm, in0=tf, scalar1=-0.5, scalar2=None, op0=ALU.is_lt),
               reads=["tmpb"], writes=["tmpb"])
            op("dve", lambda e: e.tensor_tensor(out=tf, in0=tf, in1=tm, op=ALU.add), reads=["tmpb"], writes=["tmpb"])
            op("act", lambda e, dst=dst: e.activation(out=dst, in_=tf, func=AF.Sin, scale=TWO_PI),
               reads=["tmpb"], writes=keys)

    def s5_consts():
        onesb = tmpa[:, 0:16].bitcast(BF16)[:, 0:16]
        op("pool", lambda e: e.memset(Fsel[:], 0.0), writes=["Fsel"])
        op("pool", lambda e: e.memset(onesb, 1.0), writes=["onesb"])
        for a_ in range(8):
            for b_ in range(8):
                op("pool", lambda e, a_=a_, b_=b_: e.affine_select(
                    out=Fsel[:, a_, b_, 16 * b_:16 * b_ + 16], in_=onesb, pattern=[[-1, 16]],
                    compare_op=ALU.is_equal, fill=0.0, base=-16 * a_, channel_multiplier=1),
                   reads=["onesb"], writes=["Fsel"])
        op("pool", lambda e: e.memset(mask_ts[:], 1.0), writes=["mask_ts"])
        op("pool", lambda e: e.affine_select(out=mask_ts[:].rearrange("p (t c) -> p t c", c=16),
                                            in_=mask_ts[:].rearrange("p (t c) -> p t c", c=16),
                                            pattern=[[16, 8], [0, 16]], compare_op=ALU.is_ge, fill=0.0, base=15,
                                            channel_multiplier=-1), reads=["mask_ts"], writes=["mask_ts"])
        for o in range(2):
            dma("sp", dcol[:, o, :], ssm_d[o].rearrange("(k p) -> p k", p=128), writes=["dcol"],
                allow_slow_non_contiguous=True)
        op("dve", lambda e: e.memset(Xcar[:], 0.0), writes=["Xcar"])

    def s5_setup(o):
        off = 0
        G = 128
        GB = 16

        def fb(words, parts=64):
            nonlocal off
            v = arena[0:parts, off:off + words]
            off += words
            return v
        lr, li, dt_, lrdt, turns, mag, abr, abi, den, nr, fre, fim, t0, t1 = (fb(G) for _ in range(14))
        g16 = lambda v: v.rearrange("p (g c) -> p g c", c=16)
        braw, biraw = fb(G * 16), fb(G * 16)
        brt, bit = g16(braw), g16(biraw)
        Bbr, Bbi, Crt, Cit = (g16(fb(G * 16)) for _ in range(4))
        PW = [[fb(8 * G).rearrange("p (s g) -> p s g", g=G) for _ in range(2)] for _ in range(3)]
        craw = arena[:, off:off + 16 * 64].rearrange("p (t q) -> p t q", q=64)
        off += 16 * 64
        assert off <= ARENA_F
        gm = lambda v: v.rearrange("p (g m) -> p g m", m=128)
        Lt = [gm(braw), gm(biraw)]
        xflat = x[:].rearrange("p a b -> p (a b)")
        Rt = [gm(xflat[0:64, 0:2048]), gm(xflat[0:64, 2048:4096])]
        CAt = xflat[0:64, 4096:6144].bitcast(BF16).rearrange("p (g m) -> p g m", m=256)
        Msb = gm(xflat[:, 6144:7168].bitcast(BF16))
        WTsb = gm(xflat[:, 7168:8192].bitcast(BF16))
        scr = xnT[:].rearrange("p a b -> p (a b)").bitcast(F32)[0:64, :]
        K = ["s5s"]
        slow = dict(allow_slow_non_contiguous=True)
        dma("sp", lr, lam_re[o].rearrange("g p -> p g"), writes=K, **slow)
        dma("sp", li, lam_im[o].rearrange("g p -> p g"), writes=K, **slow)
        dma("sp", dt_, log_step[o:o + 1, :].broadcast_to([64, 128]), writes=K)
        dma("sp", brt, b_re[o].rearrange("g p c -> p g c"), writes=K)
        dma("sp", bit, b_im[o].rearrange("g p c -> p g c"), writes=K)
        for (csrc, cdst) in ((c_re, Crt), (c_im, Cit)):
            dma("sp", craw, csrc[o].rearrange("(t r) q -> r t q", r=128), writes=["craw"])
            for t in range(16):
                b = next_bank()
                op("pe", lambda e, t=t, b=b: e.transpose(out=pbank[b][0:64, 0:128], in_=craw[:, t, :], identity=ident_f[:]),
                   reads=["craw", "identf"], writes=[("pb", b)])
                op("act", lambda e, t=t, b=b, cdst=cdst: e.activation(
                    out=cdst[:, t * 8:(t + 1) * 8, :], in_=pbank[b][0:64, 0:128].rearrange("p (g c) -> p g c", c=16),
                    func=AF.Copy), reads=[("pb", b)], writes=K)
        A = lambda f: op("act", f, reads=K, writes=K)
        V = lambda f: op("dve", f, reads=K, writes=K)
        A(lambda e: e.activation(out=dt_, in_=dt_, func=AF.Exp))
        V(lambda e: e.tensor_scalar(out=lr, in0=lr, scalar1=-1e-4, scalar2=None, op0=ALU.min))
        V(lambda e: e.tensor_tensor(out=lrdt, in0=lr, in1=dt_, op=ALU.mult))
        V(lambda e: e.tensor_tensor(out=turns, in0=li, in1=dt_, op=ALU.mult))
        V(lambda e: e.tensor_scalar(out=turns, in0=turns, scalar1=1.0 / TWO_PI, scalar2=None, op0=ALU.mult))
        V(lambda e: e.tensor_scalar(out=mag, in0=lrdt, scalar1=1.0 / 720.0, scalar2=None, op0=ALU.mult))
        for cf in (1.0 / 120.0, 1.0 / 24.0, 1.0 / 6.0, 0.5, 1.0):
            V(lambda e, cf=cf: e.scalar_tensor_tensor(out=mag, in0=mag, scalar=cf, in1=lrdt, op0=ALU.add, op1=ALU.mult))
        V(lambda e: e.tensor_scalar(out=mag, in0=mag, scalar1=1.0, scalar2=None, op0=ALU.add))
        ti_ = tmpa[0:64, 0:G].bitcast(I32)
        V(lambda e: e.tensor_copy(out=ti_, in_=turns))
        V(lambda e: e.tensor_copy(out=t0, in_=ti_))
        V(lambda e: e.tensor_tensor(out=t0, in0=turns, in1=t0, op=ALU.subtract))
        V(lambda e: e.tensor_scalar(out=t1, in0=t0, scalar1=0.5, scalar2=None, op0=ALU.is_gt))
        V(lambda e: e.tensor_tensor(out=t0, in0=t0, in1=t1, op=ALU.subtract))
        V(lambda e: e.tensor_scalar(out=t1, in0=t0, scalar1=-0.5, scalar2=None, op0=ALU.is_lt))
        V(lambda e: e.tensor_tensor(out=t0, in0=t0, in1=t1, op=ALU.add))
        V(lambda e: e.tensor_scalar(out=t0, in0=t0, scalar1=TWO_PI, scalar2=None, op0=ALU.mult))
        V(lambda e: e.tensor_tensor(out=t1, in0=t0, in1=t0, op=ALU.mult))
        fact = math.factorial
        V(lambda e: e.tensor_scalar(out=abi, in0=t1, scalar1=-1.0 / fact(15), scalar2=None, op0=ALU.mult))
        for kk in (6, 5, 4, 3, 2, 1):
            cf = ((-1.0) ** kk) / fact(2 * kk + 1)
            V(lambda e, cf=cf: e.scalar_tensor_tensor(out=abi, in0=abi, scalar=cf, in1=t1, op0=ALU.add, op1=ALU.mult))
        V(lambda e: e.scalar_tensor_tensor(out=abi, in0=abi, scalar=1.0, in1=t0, op0=ALU.add, op1=ALU.mult))
        V(lambda e: e.tensor_scalar(out=abr, in0=t1, scalar1=1.0 / fact(16), scalar2=None, op0=ALU.mult))
        for kk in (7, 6, 5, 4, 3, 2, 1):
            cf = ((-1.0) ** kk) / fact(2 * kk)
            V(lambda e, cf=cf: e.scalar_tensor_tensor(out=abr, in0=abr, scalar=cf, in1=t1, op0=ALU.add, op1=ALU.mult))
        V(lambda e: e.tensor_scalar(out=abr, in0=abr, scalar1=1.0, scalar2=None, op0=ALU.add))
        V(lambda e: e.tensor_tensor(out=abr, in0=abr, in1=mag, op=ALU.mult))
        V(lambda e: e.tensor_tensor(out=abi, in0=abi, in1=mag, op=ALU.mult))
        V(lambda e: e.tensor_tensor(out=den, in0=lr, in1=lr, op=ALU.mult))
        V(lambda e: e.tensor_tensor(out=t0, in0=li, in1=li, op=ALU.mult))
        V(lambda e: e.tensor_tensor(out=den, in0=den, in1=t0, op=ALU.add))
        V(lambda e: e.reciprocal(out=den, in_=den))
        V(lambda e: e.tensor_scalar(out=nr, in0=abr, scalar1=-1.0, scalar2=None, op0=ALU.add))
        V(lambda e: e.tensor_tensor(out=t0, in0=nr, in1=lr, op=ALU.mult))
        V(lambda e: e.tensor_tensor(out=t1, in0=abi, in1=li, op=ALU.mult))
        V(lambda e: e.tensor_tensor(out=fre, in0=t0, in1=t1, op=ALU.add))
        V(lambda e: e.tensor_tensor(out=fre, in0=fre, in1=den, op=ALU.mult))
        V(lambda e: e.tensor_tensor(out=t0, in0=abi, in1=lr, op=ALU.mult))
        V(lambda e: e.tensor_tensor(out=t1, in0=nr, in1=li, op=ALU.mult))
        V(lambda e: e.tensor_tensor(out=fim, in0=t0, in1=t1, op=ALU.subtract))
        V(lambda e: e.tensor_tensor(out=fim, in0=fim, in1=den, op=ALU.mult))

        def cmul(ore, oim, ar, ai, br, bi, tmp, neg_im=False):
            V(lambda e: e.tensor_tensor(out=ore, in0=ar, in1=br, op=ALU.mult))
            V(lambda e: e.tensor_tensor(out=tmp, in0=ai, in1=bi, op=ALU.mult))
            V(lambda e: e.tensor_tensor(out=ore, in0=ore, in1=tmp, op=ALU.subtract))
            V(lambda e: e.tensor_tensor(out=oim, in0=ar, in1=bi, op=ALU.mult))
            V(lambda e: e.tensor_tensor(out=tmp, in0=ai, in1=br, op=ALU.mult))
            V(lambda e: e.tensor_tensor(out=oim, in0=oim, in1=tmp, op=ALU.add))
            if neg_im:
                V(lambda e: e.tensor_scalar(out=oim, in0=oim, scalar1=-1.0, scalar2=None, op0=ALU.mult))

        bc16 = lambda v: v.unsqueeze(2).broadcast_to([64, G, 16])
        cmul(Bbr, Bbi, bc16(fre), bc16(fim), brt, bit, g16(scr[:, 0:2048]))
        PL, PR, PC = PW
        V(lambda e: e.tensor_copy(out=PC[0][:, 0, :], in_=abr))
        V(lambda e: e.tensor_copy(out=PC[1][:, 0, :], in_=abi))
        for t_ in range(1, 8):
            cmul(PC[0][:, t_, :], PC[1][:, t_, :], PC[0][:, t_ - 1, :], PC[1][:, t_ - 1, :], abr, abi, t0)
        for ri in range(2):
            V(lambda e, ri=ri: e.memset(PL[ri][:, 7, :], 1.0 - ri))
            V(lambda e, ri=ri: e.memset(PR[ri][:, 7, :], 1.0 - ri))
            for s_ in range(7):
                V(lambda e, ri=ri, s_=s_: e.tensor_copy(out=PL[ri][:, s_, :], in_=PC[ri][:, 6 - s_, :]))
        for t_ in range(7):
            pr_, pi_ = PC[0][:, 6 - t_, :], PC[1][:, 6 - t_, :]
            V(lambda e, pr_=pr_: e.tensor_tensor(out=t0, in0=pr_, in1=pr_, op=ALU.mult))
            V(lambda e, pi_=pi_: e.tensor_tensor(out=t1, in0=pi_, in1=pi_, op=ALU.mult))
            V(lambda e: e.tensor_tensor(out=t0, in0=t0, in1=t1, op=ALU.add))
            V(lambda e: e.reciprocal(out=t0, in_=t0))
            V(lambda e, pr_=pr_, t_=t_: e.tensor_tensor(out=PR[0][:, t_, :], in0=pr_, in1=t0, op=ALU.mult))
            V(lambda e, pi_=pi_, t_=t_: e.scalar_tensor_tensor(out=PR[1][:, t_, :], in0=pi_, scalar=-1.0, in1=t0,
                                                              op0=ALU.mult, op1=ALU.mult))
        op("dve", lambda e: e.tensor_copy(out=AB8[:, o, 0, :], in_=PW[2][0][:, 7, :]), reads=K, writes=["AB8"])
        op("dve", lambda e: e.tensor_copy(out=AB8[:, o, 1, :], in_=PW[2][1][:, 7, :]), reads=K, writes=["AB8"])

        v4 = lambda v: v.rearrange("p g (s c) -> p g s c", c=16)
        tmp4 = scr[:, 0:GB * 128].rearrange("p (g s c) -> p g s c", s=8, c=16)
        for g0 in range(0, G, GB):
            gs = slice(g0, g0 + GB)
            pwv = lambda ti, ri: PW[ti][ri][:, :, gs].rearrange("p s g -> p g s").unsqueeze(3).broadcast_to([64, GB, 8, 16])
            bbv = lambda v: v[:, gs, :].unsqueeze(2).broadcast_to([64, GB, 8, 16])
            cmul(v4(Lt[0]), v4(Lt[1]), pwv(0, 0), pwv(0, 1), bbv(Bbr), bbv(Bbi), tmp4)
            cmul(v4(Rt[0]), v4(Rt[1]), pwv(1, 0), pwv(1, 1), bbv(Crt), bbv(Cit), tmp4, neg_im=True)
            for gl in range(GB):
                b = next_bank()
                op("pe", lambda e, gl=gl, b=b: e.matmul(pbank[b][:, 0:128], lhsT=Lt[0][:, gl, :], rhs=Rt[0][:, gl, :],
                                                       start=True, stop=False), reads=K, writes=[("pb", b)])
                op("pe", lambda e, gl=gl, b=b: e.matmul(pbank[b][:, 0:128], lhsT=Lt[1][:, gl, :], rhs=Rt[1][:, gl, :],
                                                       start=False, stop=True), reads=K, writes=[("pb", b)])
                op("dve", lambda e, gl=gl, b=b: e.tensor_tensor(out=Msb[:, gl, :], in0=pbank[b][:, 0:128], in1=mask_ts[:],
                                                               op=ALU.mult), reads=[("pb", b), "mask_ts"], writes=["Msb"])
                b2 = next_bank()
                for ri in range(2):
                    op("pe", lambda e, gl=gl, b2=b2, ri=ri: e.transpose(out=pbank[b2][:, ri * 64:(ri + 1) * 64],
                                                                       in_=Lt[ri][:, gl, :], identity=ident_f[0:64, 0:64]),
                       reads=K + ["identf"], writes=[("pb", b2)])
                op("act", lambda e, gl=gl, b2=b2: e.activation(out=WTsb[:, gl, :], in_=pbank[b2][:, 0:128], func=AF.Copy),
                   reads=[("pb", b2)], writes=["WTsb"])
            cmul(v4(Rt[0]), v4(Rt[1]), pwv(2, 0), pwv(2, 1), bbv(Crt), bbv(Cit), tmp4, neg_im=True)
            for ri in range(2):
                op("act", lambda e, ri=ri: e.activation(out=CAt[:, :, ri * 128:(ri + 1) * 128], in_=Rt[ri][:], func=AF.Copy),
                   reads=K, writes=["CAt"])
            dma("sp", M_all[o][:, gs, :], Msb[:], reads=["Msb"], writes=[("Mall", o)])
            dma("sp", WT_all[o][:, gs, :], WTsb[:], reads=["WTsb"], writes=[("WTall", o)])
            dma("sp", CA_all[o][:, gs, :], CAt[:], reads=["CAt"], writes=[("CAall", o)])

    def s5_mixer(kind, bidx, l):
        o = l // 2
        samp = kind == "s"
        TB = 1 if samp else 4
        NB = TB * 128
        NJ = NB // 8
        last = (not samp) and bidx == cfg["last_p"]
        QG = 16
        off = 0

        def cv_(words, dt=F32, parts=128):
            nonlocal off
            v = arena[0:parts, off:off + words]
            off += words
            return v.bitcast(BF16) if dt == BF16 else v
        gyT = cv_(4096, BF16).rearrange("p (c n) -> p c n", n=512)
        Msb = cv_(QG * 64, BF16).rearrange("p (g m) -> p g m", m=128)
        WTsb = cv_(QG * 64, BF16).rearrange("p (g m) -> p g m", m=128)
        CAsb = cv_(QG * 128, BF16, 64).rearrange("p (g m) -> p g m", m=256)
        uTsb = cv_(QG * 32, BF16).rearrange("p (g j) -> p g j", j=64)
        Xh = cv_(2 * QG * 65, F32, 64).rearrange("p (r g j) -> p r g j", g=QG, j=65)
        Xb = cv_(QG * 64, BF16, 64).rearrange("p (r g j) -> p r g j", g=QG, j=64)
        Ysb = cv_(QG * 32, BF16).rearrange("p (g j) -> p g j", j=64)
        s1 = cv_(2 * QG * 16, F32, 64).rearrange("p (r g j) -> p r g j", g=QG, j=16)
        s2 = cv_(2 * QG * 16, F32, 64).rearrange("p (r g j) -> p r g j", g=QG, j=16)
        yt = cv_(512)
        assert off <= ARENA_F - 1024
        rb = arena[:, ARENA_F - 1024:ARENA_F]

        if samp:
            for (src, ri) in ((st_re, 0), (st_im, 1)):
                dma("sp", rb.rearrange("p (b q) -> p b q", q=64), src[o].rearrange("b g p -> g b p"), writes=["rowbuf"])
                for b_ in range(NSAMP):
                    f = st["pf"]
                    st["pf"] = 1 - f
                    pview = ptr_f[f][:].rearrange("p a b -> p (a b)")
                    op("pe", lambda e, b_=b_, pview=pview: e.transpose(out=pview[0:64, 0:128], in_=rb[:, b_ * 64:(b_ + 1) * 64],
                                                                     identity=ident_f[:]),
                       reads=["rowbuf", "identf"], writes=[("ptr_f", f)])
                    op("act", lambda e, b_=b_, ri=ri, pview=pview: e.activation(out=Xs_in[:, ri, :, b_], in_=pview[0:64, 0:128],
                                                                               func=AF.Copy),
                       reads=[("ptr_f", f)], writes=["Xs_in"])

        rms_to_T(TB, 4 + l)
        for qd in range(128 // QG):
            g0 = qd * QG
            gsl = slice(g0, g0 + QG)
            dma("sp", Msb[:], M_all[o][:, gsl, :], reads=[("Mall", o)], writes=["Msb"])
            dma("sp", WTsb[:], WT_all[o][:, gsl, :], reads=[("WTall", o)], writes=["WTsb"])
            dma("sp", CAsb[:], CA_all[o][:, gsl, :], reads=[("CAall", o)], writes=["CAsb"])
            for gl in range(QG):
                g = g0 + gl
                kc, q = g // 8, g % 8
                b = next_bank()
                for s_ in range(8):
                    op("pe", lambda e, b=b, q=q, s_=s_, kc=kc: e.matmul(pbank[b][:, 0:NJ], lhsT=Fsel[:, q, s_, :],
                                                                      rhs=xnT[:, kc, s_:NB:8], start=(s_ == 0), stop=(s_ == 7)),
                       reads=["Fsel", "xnT"], writes=[("pb", b)])
                op("act", lambda e, b=b, gl=gl: e.activation(out=uTsb[:, gl, 0:NJ], in_=pbank[b][:, 0:NJ], func=AF.Copy),
                   reads=[("pb", b)], writes=["uTsb"])
                b2 = next_bank()
                for ri in range(2):
                    op("pe", lambda e, b2=b2, gl=gl, ri=ri: e.matmul(pbank[b2][0:64, ri * 64:ri * 64 + NJ],
                                                                    lhsT=WTsb[:, gl, ri * 64:(ri + 1) * 64],
                                                                    rhs=uTsb[:, gl, 0:NJ], start=True, stop=True),
                       reads=["WTsb", "uTsb"], writes=[("pb", b2)])
                op("dve", lambda e, b2=b2, gl=gl: e.tensor_copy(
                    out=Xh[:, :, gl, 1:NJ + 1], in_=pbank[b2][0:64, 0:128].rearrange("p (r j) -> p r j", j=64)[:, :, 0:NJ]),
                   reads=[("pb", b2)], writes=["Xh"])
            abr_ = AB8[:, o, 0, gsl]
            abi_ = AB8[:, o, 1, gsl]
            C = dict(chain=True)
            if not samp:
                op("dve", lambda e, gsl=gsl: e.tensor_copy(out=Xh[:, :, :, 0], in_=Xcar[:, o, :, gsl]), reads=["Xcar", "Xh"], writes=["Xh"])
                t1 = s1[:, :, :, 0]
                t2 = s2[:, :, :, 0]
                for j in range(NJ):
                    prev, cur = Xh[:, :, :, j], Xh[:, :, :, j + 1]
                    op("dve", lambda e, prev=prev, abr_=abr_: e.tensor_tensor(out=t1, in0=prev, in1=abr_.unsqueeze(1).broadcast_to([64, 2, QG]),
                                                                  op=ALU.mult), reads=["Xh", "AB8"], writes=["s1"], **C)
                    op("dve", lambda e, cur=cur: e.tensor_tensor(out=t1, in0=t1, in1=cur, op=ALU.add),
                       reads=["Xh", "s1"], writes=["s1"], **C)
                    op("dve", lambda e, prev=prev, abi_=abi_: e.tensor_tensor(out=t2[:, 0, :], in0=prev[:, 1, :], in1=abi_, op=ALU.mult),
                       reads=["Xh", "AB8"], writes=["s2"], **C)
                    op("dve", lambda e, prev=prev, abi_=abi_: e.tensor_tensor(out=t2[:, 1, :], in0=prev[:, 0, :], in1=abi_, op=ALU.mult),
                       reads=["Xh", "AB8"], writes=["s2"], **C)
                    op("dve", lambda e, cur=cur: e.tensor_tensor(out=cur[:, 0, :], in0=t1[:, 0, :], in1=t2[:, 0, :],
                                                                op=ALU.subtract), reads=["s1", "s2"], writes=["Xh"], **C)
                    op("dve", lambda e, cur=cur: e.tensor_tensor(out=cur[:, 1, :], in0=t1[:, 1, :], in1=t2[:, 1, :],
                                                                op=ALU.add), reads=["s1", "s2"], writes=["Xh"], **C)
                op("dve", lambda e, gsl=gsl: e.tensor_copy(out=Xcar[:, o, :, gsl], in_=Xh[:, :, :, NJ]), reads=["Xh"], writes=["Xcar"], **C)
                op("act", lambda e: e.activation(out=Xb[:, :, :, 0:NJ], in_=Xh[:, :, :, 0:NJ], func=AF.Copy),
                   reads=["Xh"], writes=["Xb"])
            else:
                xin = Xs_in[:, :, gsl, :]
                w_ = Xh[:, :, :, 1:NJ + 1]
                bc = lambda v: v.unsqueeze(2).broadcast_to([64, QG, NSAMP])
                abr4 = abr_.unsqueeze(1).unsqueeze(3).broadcast_to([64, 2, QG, NSAMP])
                abi3 = abi_.unsqueeze(2).broadcast_to([64, QG, NSAMP])
                xo = Xs_out[:, :, gsl, :]
                op("dve", lambda e, xin=xin, abr4=abr4: e.tensor_tensor(out=s1[:], in0=xin, in1=abr4, op=ALU.mult),
                   reads=["Xs_in", "AB8"], writes=["s1"])
                op("dve", lambda e: e.tensor_tensor(out=s1[:], in0=s1[:], in1=w_, op=ALU.add), reads=["s1", "Xh"], writes=["s1"])
                op("dve", lambda e, xin=xin, abi3=abi3: e.tensor_tensor(out=s2[:, 0], in0=xin[:, 1], in1=abi3, op=ALU.mult),
                   reads=["Xs_in", "AB8"], writes=["s2"])
                op("dve", lambda e, xin=xin, abi3=abi3: e.tensor_tensor(out=s2[:, 1], in0=xin[:, 0], in1=abi3, op=ALU.mult),
                   reads=["Xs_in", "AB8"], writes=["s2"])
                op("act", lambda e, xin=xin: e.activation(out=Xb[:, :, :, 0:NJ], in_=xin, func=AF.Copy), reads=["Xs_in"], writes=["Xb"])
                op("dve", lambda e, xo=xo: e.tensor_tensor(out=xo[:, 0], in0=s1[:, 0], in1=s2[:, 0], op=ALU.subtract),
                   reads=["s1", "s2"], writes=["Xs_in"])
                op("dve", lambda e, xo=xo: e.tensor_tensor(out=xo[:, 1], in0=s1[:, 1], in1=s2[:, 1], op=ALU.add),
                   reads=["s1", "s2"], writes=["Xs_in"])
            for gl in range(QG):
                b = next_bank()
                op("pe", lambda e, b=b, gl=gl: e.matmul(pbank[b][:, 0:NJ], lhsT=Msb[:, gl, :], rhs=uTsb[:, gl, 0:NJ],
                                                       start=True, stop=False), reads=["Msb", "uTsb"], writes=[("pb", b)])
                for ri in range(2):
                    op("pe", lambda e, b=b, gl=gl, ri=ri: e.matmul(pbank[b][:, 0:NJ], lhsT=CAsb[:, gl, ri * 128:(ri + 1) * 128],
                                                                  rhs=Xb[:, ri, gl, 0:NJ], start=False, stop=(ri == 1)),
                       reads=["CAsb", "Xb"], writes=[("pb", b)])
                op("act", lambda e, b=b, gl=gl: e.activation(out=Ysb[:, gl, 0:NJ], in_=pbank[b][:, 0:NJ], func=AF.Copy),
                   reads=[("pb", b)], writes=["Ysb"])
            for kl in range(QG // 8):
                kc = qd * (QG // 8) + kl
                b = next_bank()
                for q in range(8):
                    for t in range(8):
                        op("pe", lambda e, b=b, q=q, t=t, kl=kl: e.matmul(pbank[b][:, t:NB:8], lhsT=Fsel[:, t, q, :],
                                                                         rhs=Ysb[:, kl * 8 + q, 0:NJ],
                                                                         start=(q == 0 and t == 0), stop=(q == 7 and t == 7)),
                           reads=["Fsel", "Ysb"], writes=[("pb", b)])
                op("dve", lambda e, b=b, kc=kc: e.scalar_tensor_tensor(out=yt[:, 0:NB], in0=xnT[:, kc, 0:NB],
                                                                      scalar=dcol[:, o, kc:kc + 1], in1=pbank[b][:, 0:NB],
                                                                      op0=ALU.mult, op1=ALU.add),
                   reads=["xnT", "dcol", ("pb", b)], writes=["yt"])
                gelu_tanh(yt[:, 0:NB], ["yt"], gyT[:, kc, 0:NB], ["gyT"], NB)
        for c in range(16):
            w1 = load_w(w_glu[o], 16, c * 128)
            b1 = proj(w1, 16, lambda k: gyT[:, k, 0:NB], NB, ["gyT"])
            w2 = load_w(w_glu[o], 16, D + c * 128)
            b2 = proj(w2, 16, lambda k: gyT[:, k, 0:NB], NB, ["gyT"])
            op("act", lambda e, b2=b2: e.activation(out=tmpa[:, 0:NB], in_=pbank[b2][:, 0:NB], func=AF.Sigmoid),
               reads=[("pb", b2)], writes=["tmpa"])
            op("dve", lambda e, b1=b1: e.tensor_tensor(out=tmpb[:, 0:NB], in0=pbank[b1][:, 0:NB], in1=tmpa[:, 0:NB], op=ALU.mult),
               reads=[("pb", b1), "tmpa"], writes=["tmpb"])
            add_T_into_x(tmpb, "tmpb", TB, c)
        if samp or last:
            for ri, (dp, ds) in enumerate(((o_re_p, o_re_s), (o_im_p, o_im_s))):
                nb = NSAMP if samp else 1
                for b_ in range(nb):
                    f = st["pf"]
                    st["pf"] = 1 - f
                    pview = ptr_f[f][:].rearrange("p a b -> p (a b)")
                    src = Xs_out[:, ri, :, b_] if samp else Xcar[:, o, ri, :]
                    op("pe", lambda e, src=src, pview=pview: e.transpose(out=pview[:, 0:64], in_=src, identity=ident_f[0:64, 0:64]),
                       reads=["Xs_in", "Xcar", "identf"], writes=[("ptr_f", f)])
                    op("act", lambda e, b_=b_, pview=pview: e.activation(out=rb[:, b_ * 64:(b_ + 1) * 64], in_=pview[:, 0:64],
                                                                       func=AF.Copy), reads=[("ptr_f", f)], writes=["rowbuf"])
                if samp:
                    t = dma("sp", ds[o].rearrange("b g p -> g b p"), rb.rearrange("p (b q) -> p b q", q=64), reads=["rowbuf"])
                else:
                    t = dma("sp", dp[o], rb[:, 0:64], reads=["rowbuf"])
                P.out_tokens.append(t)

    op("dve", lambda e: e.memset(stat[:, 3:4], EPS), writes=["epsb"])
    op("pool", lambda e: e.memset(ident_f[:], 1.0), writes=["identf"])
    op("pool", lambda e: e.affine_select(out=ident_f[:], in_=ident_f[:], pattern=[[-1, 128]],
                                        compare_op=ALU.is_equal, fill=0.0, base=0, channel_multiplier=1),
       reads=["identf"], writes=["identf"])
    op("dve", lambda e: e.tensor_copy(out=ident_b[:], in_=ident_f[:]), reads=["identf"], writes=["identb"])
    for g in range(13):
        dma("sp", gcol[:, g, :], norms[g].rearrange("(k p) -> p k", p=128), writes=["gcol"],
            allow_slow_non_contiguous=True)
    if do_mix and evens:
        op("dve", lambda e: e.memset(ones_c[:], 1.0), writes=["ones"])
        even_setup()
    P.barrier()
    if do_mix and odds:
        s5_consts()
        P.barrier()
        for l in odds:
            s5_setup(l // 2)
            P.barrier()

    for (kind, bidx) in cfg["blocks"]:
        if kind == "p":
            xin, yo, TB = xp[bidx * 512:(bidx + 1) * 512, :], yp[bidx * 512:(bidx + 1) * 512, :], 4
        else:
            xin, yo, TB = xs_in, ys, 1
        for i in range(TB):
            dma("sp", x[:, i, :], xin[i * 128:(i + 1) * 128, :], writes=[("x", i)])
        for l in layers:
            if do_ffn:
                ffn(TB, l, w1i[l], w1o[l])
                P.barrier()
            if do_mix:
                if l % 2 == 0:
                    even_mixer(kind, bidx, l)
                else:
                    s5_mixer(kind, bidx, l)
                P.barrier()
            if do_ffn:
                ffn(TB, 8 + l, w2i[l], w2o[l])
                P.barrier()
        final_norm_store(TB, yo, cfg["final"])
        P.barrier()

    P.emit("sp", "nop", extra=P.out_tokens)

    def sem_of(tok):
        kind, key, val = tok
        return (sems[key] if kind == "eng" else dsems[key]), val

    def replay(name, eng):
        for waits, fn, tok in P.ops[name]:
            for w in waits:
                s, v = sem_of(w)
                eng.wait_ge(s, v)
            if fn == "nop":
                continue
            s, v = sem_of(tok)
            ins = fn(eng)
            ins.then_inc(s, 16 if tok[0] == "dma" else 1)

    with nc.Block() as block:
        @block.tensor
        def _(e):
            replay("pe", e)

        @block.scalar
        def _(e):
            replay("act", e)

        @block.vector
        def _(e):
            replay("dve", e)

        @block.gpsimd
        def _(e):
            replay("pool", e)

        @block.sync
        def _(e):
            replay("sp", e)

    es.close()
    return nc, in_names


_NC_CACHE = {}


def make_in_maps(inp, in_names):
    f = lambda a: np.ascontiguousarray(np.asarray(a, dtype=np.float32))
    norms = np.concatenate([f(inp["norm_ffn1"]), f(inp["norm_mix"]), f(inp["norm_ffn2"]),
                            f(inp["final_norm"])[None, :]], axis=0)
    xpr = f(inp["x_prompt"])
    xsa = f(inp["x_sample"]).reshape(128 * 8, D)
    evec = np.concatenate([f(inp["pool_scale"])[:, None, :], f(inp["conv_w"]), f(inp["conv_b"])[:, None, :],
                           f(inp["gate_a_b"])[:, None, :], f(inp["gate_x_b"])[:, None, :],
                           f(inp["rglru_lambda"])[:, None, :]], axis=1)
    shared = {"norms": norms, "evec": evec}
    for k in ("w_ffn1_in", "w_ffn1_out", "w_ffn2_in", "w_ffn2_out", "w_in_even", "pool_w", "gate_a_w", "gate_x_w",
              "w_out_even", "ssm_lambda_re", "ssm_lambda_im", "ssm_log_step", "ssm_b_re", "ssm_b_im", "ssm_d", "w_glu"):
        if k in in_names:
            shared[k] = f(inp[k])
    if "ssm_c_re" in in_names:
        shared["ssm_c_re"] = f(inp["ssm_c_re"]).reshape(2, 2048, 64)
        shared["ssm_c_im"] = f(inp["ssm_c_im"]).reshape(2, 2048, 64)
    in_maps = []
    for c in range(NCORES):
        m = {}
        sl = slice(c * NSAMP, (c + 1) * NSAMP)
        per = {"xp": xpr[c % 4], "xs": xsa[c * 128:(c + 1) * 128]}
        if "st_pool" in in_names:
            per["st_pool"] = f(inp["state_pool"])[:, sl].reshape(2, NSAMP * 15, 1024)
            per["st_conv"] = f(inp["state_conv"])[:, sl].reshape(2, NSAMP * 3, 1024)
            per["st_h"] = f(inp["state_rglru"])[:, sl]
        if "st_re" in in_names:
            per["st_re"] = f(inp["state_ssm_re"])[:, sl]
            per["st_im"] = f(inp["state_ssm_im"])[:, sl]
        for k in in_names:
            m[k] = np.ascontiguousarray(per[k]) if k in per else shared[k]
        in_maps.append(m)
    return in_maps


def gather(r):
    y_prompt = np.stack([r[c]["yp"] for c in range(4)], 0)
    y_sample = np.concatenate([r[c]["ys"] for c in range(NCORES)], 0).reshape(128, 8, D)
    pp = np.stack([r[c]["o_pool_p"] for c in range(4)], 1)
    psm = np.concatenate([r[c]["o_pool_s"].reshape(2, NSAMP, 15, 1024) for c in range(NCORES)], 1)
    cp = np.stack([r[c]["o_conv_p"] for c in range(4)], 1)
    csm = np.concatenate([r[c]["o_conv_s"].reshape(2, NSAMP, 3, 1024) for c in range(NCORES)], 1)
    hp = np.stack([r[c]["o_h_p"].reshape(2, 1024) for c in range(4)], 1)
    hsm = np.concatenate([r[c]["o_h_s"] for c in range(NCORES)], 1)
    rp = np.stack([r[c]["o_re_p"] for c in range(4)], 1)
    rs = np.concatenate([r[c]["o_re_s"] for c in range(NCORES)], 1)
    ip = np.stack([r[c]["o_im_p"] for c in range(4)], 1)
    is_ = np.concatenate([r[c]["o_im_s"] for c in range(NCORES)], 1)
    return (y_prompt, y_sample, pp, psm, cp, csm, hp, hsm, rp, rs, ip, is_)


def kernel(**inp):
    if "nc" not in _NC_CACHE:
        _NC_CACHE["nc"] = build_program()
    nc, in_names = _NC_CACHE["nc"]
    in_maps = make_in_maps(inp, in_names)
    res = run_bass_kernel_spmd(nc, in_maps, core_ids=list(range(NCORES)))
    return gather(res.results)
```

```python
import contextlib
import math
import numpy as np
import concourse.bass as bass
import concourse.mybir as mybir
from concourse.bass_utils import run_bass_kernel_spmd

F32 = mybir.dt.float32
BF16 = mybir.dt.bfloat16
I32 = mybir.dt.int32
AF = mybir.ActivationFunctionType
ALU = mybir.AluOpType

D = 2048
DFF = 5504
NFC = DFF // 128
DEPTH = 4
SEQ = 2048
NCORES = 8
NSAMP = 16
EPS = 1e-6
KMAX = 43
TWO_PI = 2.0 * math.pi

ENGS = ("pe", "act", "dve", "pool", "sp")

FULL_CFG = dict(blocks=[("p", 0), ("p", 1), ("p", 2), ("p", 3), ("s", 0)], layers=[0, 1, 2, 3],
                ffn=True, mixer=True, final=True, last_p=3)


class Prog:
    def __init__(self):
        self.ops = {e: [] for e in ENGS}
        self.cnt = {e: 0 for e in ENGS}
        self.waited = {e: {} for e in ENGS}
        self.buf = {}
        self.ndma = 24
        self.dma_val = {}
        self.dma_rr = {e: 0 for e in ENGS}
        self.out_tokens = []
        self.dma_live = []

    def _wait(self, e, tok, waits, force=False):
        kind, key, val = tok
        if kind == "eng" and key == e and not force:
            return
        k = (kind, key)
        if self.waited[e].get(k, 0) >= val:
            return
        self.waited[e][k] = val
        waits.append(tok)

    def emit(self, e, fn, reads=(), writes=(), dma=False, extra=(), hard=()):
        deps = list(extra)
        hard = list(hard)
        for k in reads:
            b = self.buf.get(k)
            if b and b[0]:
                (hard if e != "pe" else deps).append(b[0])
        for k in writes:
            b = self.buf.get(k)
            if b:
                tgt = hard if e != "pe" else deps
                if b[0]:
                    tgt.append(b[0])
                tgt.extend(b[1])
        waits = []
        for d in deps:
            self._wait(e, d, waits)
        for d in hard:
            self._wait(e, d, waits, force=True)
        if dma:
            idx = self.dma_rr[e]
            self.dma_rr[e] = (idx + 1) % self.ndma
            prev = self.dma_val.get((e, idx), 0)
            if prev:
                self._wait(e, ("dma", (e, idx), prev), waits)
            val = prev + 16
            self.dma_val[(e, idx)] = val
            tok = ("dma", (e, idx), val)
            self.dma_live.append(tok)
        elif fn == "nop":
            tok = ("eng", e, self.cnt[e])
        else:
            self.cnt[e] += 1
            tok = ("eng", e, self.cnt[e])
        self.ops[e].append((waits, fn, tok))
        for k in reads:
            self.buf.setdefault(k, [None, []])[1].append(tok)
        for k in writes:
            self.buf[k] = [tok, []]
        return tok

    def barrier(self):
        toks = [("eng", e, self.cnt[e]) for e in ENGS if self.cnt[e] > 0] + self.dma_live
        self.dma_live = []
        for e in ("pe", "act", "dve", "pool", "sp"):
            self.emit(e, "nop", extra=toks)


def build_program(cfg=None):
    cfg = cfg or FULL_CFG
    nc = bass.Bass("TRN2", target_bir_lowering=False)
    P = Prog()
    es = contextlib.ExitStack()
    in_names = []

    def dram(name, shape, kind="ExternalInput", dt=F32):
        if kind == "ExternalInput":
            in_names.append(name)
        return nc.dram_tensor(name, list(shape), dt, kind=kind).ap()

    layers = cfg["layers"]
    do_ffn, do_mix = cfg["ffn"], cfg["mixer"]
    evens = [l for l in layers if l % 2 == 0]
    odds = [l for l in layers if l % 2 == 1]

    xp = dram("xp", [SEQ, D])
    xs_in = dram("xs", [NSAMP * 8, D])
    norms = dram("norms", [13, D])
    yp = dram("yp", [SEQ, D], kind="ExternalOutput")
    ys = dram("ys", [NSAMP * 8, D], kind="ExternalOutput")
    if do_ffn:
        w1i = dram("w_ffn1_in", [DEPTH, D, 2 * DFF])
        w1o = dram("w_ffn1_out", [DEPTH, DFF, D])
        w2i = dram("w_ffn2_in", [DEPTH, D, 2 * DFF])
        w2o = dram("w_ffn2_out", [DEPTH, DFF, D])
    if do_mix and evens:
        w_in_even = dram("w_in_even", [2, D, 3072])
        pool_w = dram("pool_w", [2, 4, 256, 256])
        evec = dram("evec", [2, 9, 1024])
        gate_a_w = dram("gate_a_w", [2, 8, 128, 128])
        gate_x_w = dram("gate_x_w", [2, 8, 128, 128])
        w_out_even = dram("w_out_even", [2, D, D])
        st_pool = dram("st_pool", [2, NSAMP * 15, 1024])
        st_conv = dram("st_conv", [2, NSAMP * 3, 1024])
        st_h = dram("st_h", [2, NSAMP, 1024])
    o_pool_p = dram("o_pool_p", [2, 15, 1024], kind="ExternalOutput")
    o_pool_s = dram("o_pool_s", [2, NSAMP * 15, 1024], kind="ExternalOutput")
    o_conv_p = dram("o_conv_p", [2, 3, 1024], kind="ExternalOutput")
    o_conv_s = dram("o_conv_s", [2, NSAMP * 3, 1024], kind="ExternalOutput")
    o_h_p = dram("o_h_p", [2, 1, 1024], kind="ExternalOutput")
    o_h_s = dram("o_h_s", [2, NSAMP, 1024], kind="ExternalOutput")
    o_re_p = dram("o_re_p", [2, 128, 64], kind="ExternalOutput")
    o_re_s = dram("o_re_s", [2, NSAMP, 128, 64], kind="ExternalOutput")
    o_im_p = dram("o_im_p", [2, 128, 64], kind="ExternalOutput")
    o_im_s = dram("o_im_s", [2, NSAMP, 128, 64], kind="ExternalOutput")
    if do_mix and odds:
        lam_re = dram("ssm_lambda_re", [2, 128, 64])
        lam_im = dram("ssm_lambda_im", [2, 128, 64])
        log_step = dram("ssm_log_step", [2, 128])
        b_re = dram("ssm_b_re", [2, 128, 64, 16])
        b_im = dram("ssm_b_im", [2, 128, 64, 16])
        c_re = dram("ssm_c_re", [2, 128 * 16, 64])
        c_im = dram("ssm_c_im", [2, 128 * 16, 64])
        ssm_d = dram("ssm_d", [2, D])
        w_glu = dram("w_glu", [2, D, 2 * D])
        st_re = dram("st_re", [2, NSAMP, 128, 64])
        st_im = dram("st_im", [2, NSAMP, 128, 64])
        M_all = [dram(f"M_all{o}", [128, 128, 128], kind="Internal", dt=BF16) for o in range(2)]
        WT_all = [dram(f"WT_all{o}", [128, 128, 128], kind="Internal", dt=BF16) for o in range(2)]
        CA_all = [dram(f"CA_all{o}", [64, 128, 256], kind="Internal", dt=BF16) for o in range(2)]

    def sb(name, shape, dt=F32):
        return es.enter_context(nc.sbuf_tensor(name, list(shape), dt))

    def ps(name, shape, dt=F32):
        return es.enter_context(nc.psum_tensor(name, list(shape), dt))

    x = sb("x", [128, 4, D])
    xsc = sb("xsc", [128, D], BF16)
    xnT = sb("xnT", [128, 16, 512], BF16)
    ARENA_F = 21504
    arena = sb("arena", [128, ARENA_F])
    NW = 8
    WSLOT = 2048
    wsl = [sb(f"w{i}", [128, WSLOT], BF16) for i in range(NW)]
    gcol = sb("gcol", [128, 13, 16])
    ident_b = sb("ident_b", [128, 128], BF16)
    ident_f = sb("ident_f", [128, 128])
    stat = sb("stat", [128, 8])
    tmpa = sb("tmpa", [128, 512])
    tmpb = sb("tmpb", [128, 512])
    NPB = 4
    pbank = [ps(f"pb{i}", [128, 512]) for i in range(NPB)]
    ptr_b = ps("ptr_b", [128, 4, 128], BF16)
    ptr_f = [ps(f"ptr_f{i}", [128, 4, 128]) for i in range(2)]

    def carve(off, words, shape=None, dt=F32):
        v = arena[:, off:off + words]
        if dt == BF16:
            v = v.bitcast(BF16)
        return v

    hT = carve(0, NFC * 256, dt=BF16).rearrange("p (c n) -> p c n", n=512)

    sems = {e: es.enter_context(nc.semaphore(f"s_{e}")) for e in ENGS}
    dsems = {(e, i): es.enter_context(nc.semaphore(f"d_{e}{i}")) for e in ("sp", "pool") for i in range(P.ndma)}

    st = {"w": 0, "pb": 0, "pf": 0}

    def dma(q, out, in_, reads=(), writes=(), **kw):
        return P.emit(q, lambda e: e.dma_start(out=out, in_=in_, **kw), reads=reads, writes=writes, dma=True)

    def op(e, f, reads=(), writes=(), chain=False, **kw):
        if chain and P.cnt[e] > 0:
            kw["hard"] = list(kw.get("hard", ())) + [("eng", e, P.cnt[e])]
        return P.emit(e, f, reads=reads, writes=writes, **kw)

    def load_w(w2d, K, col0, ncols=128):
        i = st["w"]
        st["w"] = (i + 1) % NW
        assert K * ncols <= WSLOT
        src = w2d[:, col0:col0 + ncols].rearrange("(k p) c -> p k c", p=128)
        view = wsl[i][:, 0:K * ncols].rearrange("p (k c) -> p k c", c=ncols)
        dma("pool", view, src, writes=[("w", i)])
        return (i, view)

    def next_bank():
        b = st["pb"]
        st["pb"] = (b + 1) % NPB
        return b

    def proj(wh, K, rhs_fn, NB, rhs_keys, sub=0):
        wi, view = wh
        b = next_bank()
        for k in range(K):
            op("pe", (lambda k: lambda e: e.matmul(pbank[b][:, 0:NB], lhsT=view[:, k, sub * 128:(sub + 1) * 128], rhs=rhs_fn(k),
                                                  start=(k == 0), stop=(k == K - 1)))(k),
               reads=[("w", wi)] + list(rhs_keys), writes=[("pb", b)])
        return b

    def rstd_of_tile(i):
        op("act", lambda e: e.activation(out=xsc[:], in_=x[:, i, :], func=AF.Square, accum_out=stat[:, 0:1]),
           reads=[("x", i)], writes=["xsc", ("stat", 0)])
        op("act", lambda e: e.activation(out=stat[:, 1:2], in_=stat[:, 0:1], func=AF.Sqrt,
                                         scale=1.0 / D, bias=stat[:, 3:4]),
           reads=[("stat", 0), "epsb"], writes=[("stat", 1)], hard=[P.buf[("stat", 0)][0]])
        op("dve", lambda e: e.reciprocal(out=stat[:, 2:3], in_=stat[:, 1:2]),
           reads=[("stat", 1)], writes=[("stat", 2)])

    def rms_to_T(TB, gi):
        for i in range(TB):
            rstd_of_tile(i)
            op("act", lambda e, i=i: e.activation(out=xsc[:], in_=x[:, i, :], func=AF.Copy, scale=stat[:, 2:3]),
               reads=[("x", i), ("stat", 2)], writes=["xsc"])
            for q in range(4):
                for j in range(4):
                    kc = q * 4 + j
                    op("pe", lambda e, kc=kc, j=j: e.transpose(out=ptr_b[:, j, :], in_=xsc[:, kc * 128:(kc + 1) * 128],
                                                             identity=ident_b[:]),
                       reads=["xsc", "identb"], writes=["ptr_b"])
                op("dve", lambda e, q=q, i=i: e.tensor_tensor(
                    out=xnT[:, q * 4:q * 4 + 4, i * 128:(i + 1) * 128], in0=ptr_b[:],
                    in1=gcol[:, gi, q * 4:q * 4 + 4].unsqueeze(2).broadcast_to([128, 4, 128]), op=ALU.mult),
                   reads=["ptr_b", "gcol"], writes=["xnT"])

    def add_T_into_x(src, skey, TB, dcol):
        f = st["pf"]
        st["pf"] = 1 - f
        for i in range(TB):
            op("pe", lambda e, i=i: e.transpose(out=ptr_f[f][:, i, :], in_=src[:, i * 128:(i + 1) * 128],
                                               identity=ident_f[:]),
               reads=[skey, "identf"], writes=[("ptr_f", f)])
        op("dve", lambda e: e.tensor_tensor(out=x[:, 0:TB, dcol * 128:(dcol + 1) * 128],
                                           in0=ptr_f[f][:, 0:TB, :],
                                           in1=x[:, 0:TB, dcol * 128:(dcol + 1) * 128], op=ALU.add),
           reads=[("ptr_f", f)], writes=[("x", i) for i in range(TB)])

    def rows_out(src_fn, nrows, ntiles, dst, skeys):
        rowbuf = tmpa if nrows <= 128 else None
        rb = arena[:, ARENA_F - 1024:ARENA_F]
        for ct in range(ntiles):
            f = st["pf"]
            st["pf"] = 1 - f
            pview = ptr_f[f][:].rearrange("p a b -> p (a b)")
            op("pe", lambda e, ct=ct, pview=pview: e.transpose(out=pview[0:nrows, 0:128], in_=src_fn(ct),
                                                             identity=ident_f[:]),
               reads=list(skeys) + ["identf"], writes=[("ptr_f", f)])
            op("act", lambda e, ct=ct, pview=pview: e.activation(out=rb[0:nrows, ct * 128:(ct + 1) * 128],
                                                               in_=pview[0:nrows, 0:128], func=AF.Copy),
               reads=[("ptr_f", f)], writes=["rowbuf"])
        t = dma("sp", dst, rb[0:nrows, 0:ntiles * 128], reads=["rowbuf"])
        P.out_tokens.append(t)

    def rows_in(src, nrows, ntiles, dst_fn, dkeys):
        rb = arena[:, ARENA_F - 1024:ARENA_F]
        dma("sp", rb[0:nrows, 0:ntiles * 128], src, writes=["rowbuf"])
        for ct in range(ntiles):
            f = st["pf"]
            st["pf"] = 1 - f
            pview = ptr_f[f][:].rearrange("p a b -> p (a b)")
            op("pe", lambda e, ct=ct, pview=pview: e.transpose(out=pview[:, 0:nrows],
                                                             in_=rb[0:nrows, ct * 128:(ct + 1) * 128],
                                                             identity=ident_f[0:nrows, 0:nrows]),
               reads=["rowbuf", "identf"], writes=[("ptr_f", f)])
            op("act", lambda e, ct=ct, pview=pview: e.activation(out=dst_fn(ct), in_=pview[:, 0:nrows], func=AF.Copy),
               reads=[("ptr_f", f)], writes=list(dkeys))

    def gelu_tanh(src, skeys, out, okeys, NB, mul_in=None, mkeys=()):
        t1 = tmpa[:, 0:NB]
        t2 = tmpb[:, 0:NB]
        op("act", lambda e: e.activation(out=t1, in_=src, func=AF.Square), reads=skeys, writes=["tmpa"])
        op("dve", lambda e: e.tensor_scalar(out=t1, in0=t1, scalar1=0.044715, scalar2=1.0, op0=ALU.mult, op1=ALU.add),
           reads=["tmpa"], writes=["tmpa"])
        op("dve", lambda e: e.tensor_tensor(out=t1, in0=src, in1=t1, op=ALU.mult), reads=list(skeys) + ["tmpa"],
           writes=["tmpa"])
        op("act", lambda e: e.activation(out=t2, in_=t1, func=AF.Sigmoid, scale=1.5957691216057308),
           reads=["tmpa"], writes=["tmpb"])
        if mul_in is None:
            op("dve", lambda e: e.tensor_tensor(out=out, in0=src, in1=t2, op=ALU.mult),
               reads=list(skeys) + ["tmpb"], writes=okeys)
        else:
            op("dve", lambda e: e.tensor_tensor(out=t2, in0=src, in1=t2, op=ALU.mult),
               reads=list(skeys) + ["tmpb"], writes=["tmpb"])
            op("dve", lambda e: e.tensor_tensor(out=out, in0=t2, in1=mul_in, op=ALU.mult),
               reads=["tmpb"] + list(mkeys), writes=okeys)

    def ffn(TB, gi, w_in, w_out):
        NB = TB * 128
        rms_to_T(TB, gi)
        xr = lambda k: xnT[:, k, 0:NB]
        for c in range(NFC):
            wg = load_w(w_in, 16, c * 128)
            wu = load_w(w_in, 16, DFF + c * 128)
            bg = proj(wg, 16, xr, NB, ["xnT"])
            bu = proj(wu, 16, xr, NB, ["xnT"])
            op("act", lambda e, bg=bg: e.activation(out=tmpa[:, 0:NB], in_=pbank[bg][:, 0:NB], func=AF.Silu),
               reads=[("pb", bg)], writes=["tmpa"])
            op("dve", lambda e, bu=bu, c=c: e.tensor_tensor(out=hT[:, c, 0:NB], in0=pbank[bu][:, 0:NB],
                                                           in1=tmpa[:, 0:NB], op=ALU.mult),
               reads=[("pb", bu), "tmpa"], writes=["hT"])
        for dc0 in range(0, 16, 2):
            banks = (next_bank(), next_bank())
            kb = 0
            for K_ in (8, 8, 8, 8, 8, 3):
                wi, view = load_w(w_out[kb * 128:(kb + K_) * 128, :], K_, dc0 * 128, ncols=256)
                for k in range(K_):
                    for j, b in enumerate(banks):
                        op("pe", lambda e, k=k, j=j, b=b, kb=kb, view=view: e.matmul(
                            pbank[b][:, 0:NB], lhsT=view[:, k, j * 128:(j + 1) * 128], rhs=hT[:, kb + k, 0:NB],
                            start=(kb + k == 0), stop=(kb + k == NFC - 1)),
                           reads=[("w", wi), "hT"], writes=[("pb", b)])
                kb += K_
            for j, b in enumerate(banks):
                op("act", lambda e, b=b: e.activation(out=tmpb[:, 0:NB], in_=pbank[b][:, 0:NB], func=AF.Copy, scale=0.5),
                   reads=[("pb", b)], writes=["tmpb"])
                add_T_into_x(tmpb, "tmpb", TB, dc0 + j)

    def final_norm_store(TB, out_ap, do_norm):
        gfin = arena[:, 0:D]
        yout = arena[:, D:2 * D]
        if do_norm:
            dma("sp", gfin, norms[12:13, :].broadcast_to([128, D]), writes=["gfin"])
        for i in range(TB):
            if do_norm:
                rstd_of_tile(i)
                op("dve", lambda e, i=i: e.scalar_tensor_tensor(out=yout, in0=x[:, i, :], scalar=stat[:, 2:3],
                                                               in1=gfin, op0=ALU.mult, op1=ALU.mult),
                   reads=[("x", i), ("stat", 2), "gfin"], writes=["yout"])
                t = dma("sp", out_ap[i * 128:(i + 1) * 128, :], yout, reads=["yout"])
            else:
                t = dma("sp", out_ap[i * 128:(i + 1) * 128, :], x[:, i, :], reads=[("x", i)])
            P.out_tokens.append(t)

    if do_mix and evens:
        evc = sb("evc", [128, 2, 9, 8])
        clam = sb("clam", [128, 2, 8])
        cfix = sb("cfix", [128, 4, 15])
        ones_c = sb("ones_c", [128, 1])
        poolbuf_p = sb("poolbuf_p", [128, 2, 8, 15])
        convbuf_p = sb("convbuf_p", [128, 2, 8, 3])
        hst_p = sb("hst_p", [128, 2, 8])
        poolbuf_s = x[:].rearrange("p a b -> p (a b)")[:, 6144:6144 + 8 * NSAMP * 15].rearrange("p (c b r) -> p c b r", b=NSAMP, r=15)
        convbuf_s = sb("convbuf_s", [128, 8, NSAMP, 3])
        hst_s = sb("hst_s", [128, 8, NSAMP])

    def even_setup():
        for e in range(2):
            for v in range(9):
                dma("sp", evc[:, e, v, :], evec[e, v].rearrange("(k p) -> p k", p=128), writes=["evc"],
                    allow_slow_non_contiguous=True)
        op("act", lambda e: e.activation(out=clam[:], in_=evc[:, :, 8, :], func=AF.Exp, scale=-1.0),
           reads=["evc"], writes=["clam"])
        op("act", lambda e: e.activation(out=clam[:], in_=clam[:], func=AF.Ln, bias=ones_c[:, 0:1]),
           reads=["clam", "ones"], writes=["clam"])
        op("dve", lambda e: e.tensor_scalar(out=clam[:], in0=clam[:], scalar1=-8.0, scalar2=None, op0=ALU.mult),
           reads=["clam"], writes=["clam"])
        for g in range(4):
            w = float(2 << g)
            for pos in range(15):
                op("dve", lambda e, g=g, pos=pos, w=w: e.memset(cfix[:, g, pos:pos + 1], w / min(pos + 1.0, w)),
                   writes=["cfix"])
        op("dve", lambda e: e.memset(poolbuf_p[:], 0.0), writes=["poolbuf_p"])
        op("dve", lambda e: e.memset(convbuf_p[:], 0.0), writes=["convbuf_p"])
        op("dve", lambda e: e.memset(hst_p[:], 0.0), writes=["hst_p"])

    def even_mixer(kind, bidx, l):
        e = l // 2
        samp = kind == "s"
        TB = 1 if samp else 4
        NB = TB * 128
        nseg, L = (NSAMP, 8) if samp else (1, 512)
        first = (not samp) and bidx == 0
        last = (not samp) and bidx == cfg["last_p"]
        catT = carve(0, 16 * 256, dt=BF16).rearrange("p (c n) -> p c n", n=512)
        off = 16 * 256
        def fbuf(words):
            nonlocal off
            v = arena[:, off:off + words]
            off += words
            return v
        FW = 544
        Fb, sA, sB = fbuf(FW), fbuf(FW), fbuf(FW)
        diffT = carve(off, 512, dt=BF16).rearrange("p (c n) -> p c n", n=512); off += 512
        cv, rg, ig, ag, mg, hs = (fbuf(512) for _ in range(6))
        cvb = carve(off, 256, dt=BF16); off += 256

        def seg(v, width):
            return v[:, 0:nseg * width].rearrange("p (s w) -> p s w", w=width)

        if samp:
            pbuf = lambda ct: poolbuf_s[:, ct, :, :]
            cbuf = lambda h: convbuf_s[:, h, :, :]
            hst = lambda h: hst_s[:, h, :]
            pk, ck, hk = "poolbuf_s", "convbuf_s", "hst_s"
            rows_in(st_pool[e, 0:120, :], 120, 8,
                    lambda ct: poolbuf_s[:, ct, 0:8, :].rearrange("p b r -> p (b r)"), [pk])
            rows_in(st_pool[e, 120:240, :], 120, 8,
                    lambda ct: poolbuf_s[:, ct, 8:16, :].rearrange("p b r -> p (b r)"), [pk])
            rows_in(st_conv[e], 48, 8, lambda ct: convbuf_s[:, ct, :, :].rearrange("p b r -> p (b r)"), [ck])
            rows_in(st_h[e], 16, 8, lambda ct: hst_s[:, ct, :], [hk])
        else:
            pbuf = lambda ct: poolbuf_p[:, e, ct, :].unsqueeze(1)
            cbuf = lambda h: convbuf_p[:, e, h, :].unsqueeze(1)
            hst = lambda h: hst_p[:, e, h:h + 1]
            pk, ck, hk = "poolbuf_p", "convbuf_p", "hst_p"

        rms_to_T(TB, 4 + l)
        zrhs = lambda k: xnT[:, k, 0:NB]
        W15 = 15 + L
        F3 = seg(Fb, W15)
        for g in range(4):
            w = 2 << g
            for j2 in range(2):
                ct = 2 * g + j2
                wi = load_w(w_in_even[e], 16, ct * 128)
                bz = proj(wi, 16, zrhs, NB, ["xnT"])
                op("act", lambda e_, ct=ct: e_.activation(out=F3[:, :, 0:15], in_=pbuf(ct), func=AF.Copy),
                   reads=[pk], writes=["Fb"])
                op("act", lambda e_, bz=bz: e_.activation(out=F3[:, :, 15:W15],
                                                         in_=pbank[bz][:, 0:NB].rearrange("p (s w) -> p s w", w=L),
                                                         func=AF.Copy),
                   reads=[("pb", bz)], writes=["Fb"])
                cur, ckey, m = F3, "Fb", 1
                for lev in range(g + 1):
                    nx = seg(sA if lev % 2 == 0 else sB, W15)
                    nkey = "sA" if lev % 2 == 0 else "sB"
                    op("dve", lambda e_, cur=cur, nx=nx, m=m: e_.tensor_tensor(
                        out=nx[:, :, 2 * m - 1:W15], in0=cur[:, :, 2 * m - 1:W15], in1=cur[:, :, m - 1:W15 - m],
                        op=ALU.add), reads=[ckey], writes=[nkey])
                    cur, ckey, m = nx, nkey, 2 * m
                if first:
                    op("dve", lambda e_, cur=cur, g=g: e_.tensor_tensor(out=cur[:, 0, 15:30], in0=cur[:, 0, 15:30],
                                                                       in1=cfix[:, g, :], op=ALU.mult),
                       reads=[ckey, "cfix"], writes=[ckey])
                op("dve", lambda e_, cur=cur, j2=j2, w=w: e_.scalar_tensor_tensor(
                    out=diffT[:, j2, 0:NB].rearrange("p (s w) -> p s w", w=L), in0=cur[:, :, 15:W15],
                    scalar=1.0 / w, in1=F3[:, :, 15:W15], op0=ALU.mult, op1=ALU.subtract),
                   reads=[ckey, "Fb"], writes=["diffT"])
                op("act", lambda e_, ct=ct: e_.activation(out=pbuf(ct), in_=F3[:, :, L:L + 15], func=AF.Copy),
                   reads=["Fb"], writes=[pk])
            for j in range(2):
                wi = load_w(pool_w[e, g], 2, j * 128)
                bp = proj(wi, 2, lambda k: diffT[:, k, 0:NB], NB, ["diffT"])
                op("act", lambda e_, bp=bp, g=g, j=j: e_.activation(out=catT[:, 2 * g + j, 0:NB], in_=pbank[bp][:, 0:NB],
                                                                   func=AF.Copy, scale=evc[:, e, 0, 2 * g + j:2 * g + j + 1]),
                   reads=[("pb", bp), "evc"], writes=["catT"])
        W3 = 3 + L
        C3 = seg(Fb, W3)
        cv3, r3, i3, a3, m3, h3 = (seg(v, L) for v in (cv, rg, ig, ag, mg, hs))
        for h in range(8):
            wi = load_w(w_in_even[e], 16, (8 + h) * 128)
            bz = proj(wi, 16, zrhs, NB, ["xnT"])
            op("act", lambda e_, h=h: e_.activation(out=C3[:, :, 0:3], in_=cbuf(h), func=AF.Copy),
               reads=[ck], writes=["Fb"])
            op("act", lambda e_, bz=bz: e_.activation(out=C3[:, :, 3:W3],
                                                     in_=pbank[bz][:, 0:NB].rearrange("p (s w) -> p s w", w=L),
                                                     func=AF.Copy),
               reads=[("pb", bz)], writes=["Fb"])
            op("dve", lambda e_, h=h: e_.tensor_scalar(out=cv3, in0=C3[:, :, 3:W3], scalar1=evc[:, e, 4, h:h + 1],
                                                      scalar2=evc[:, e, 5, h:h + 1], op0=ALU.mult, op1=ALU.add),
               reads=["Fb", "evc"], writes=["cv"])
            for k in (2, 1, 0):
                op("dve", lambda e_, h=h, k=k: e_.scalar_tensor_tensor(out=cv3, in0=C3[:, :, k:k + L],
                                                                      scalar=evc[:, e, 1 + k, h:h + 1], in1=cv3,
                                                                      op0=ALU.mult, op1=ALU.add),
                   reads=["Fb", "evc", "cv"], writes=["cv"])
            op("act", lambda e_, h=h: e_.activation(out=cbuf(h), in_=C3[:, :, L:L + 3], func=AF.Copy),
               reads=["Fb"], writes=[ck])
            op("act", lambda e_: e_.activation(out=cvb[:, 0:NB], in_=cv[:, 0:NB], func=AF.Copy),
               reads=["cv"], writes=["cvb"])
            wa = load_w(gate_a_w[e, h], 1, 0)
            ba = proj(wa, 1, lambda k: cvb[:, 0:NB], NB, ["cvb"])
            op("act", lambda e_, ba=ba, h=h: e_.activation(out=rg[:, 0:NB], in_=pbank[ba][:, 0:NB], func=AF.Sigmoid,
                                                          bias=evc[:, e, 6, h:h + 1]),
               reads=[("pb", ba), "evc"], writes=["rg"])
            wx = load_w(gate_x_w[e, h], 1, 0)
            bx = proj(wx, 1, lambda k: cvb[:, 0:NB], NB, ["cvb"])
            op("act", lambda e_, bx=bx, h=h: e_.activation(out=ig[:, 0:NB], in_=pbank[bx][:, 0:NB], func=AF.Sigmoid,
                                                          bias=evc[:, e, 7, h:h + 1]),
               reads=[("pb", bx), "evc"], writes=["ig"])
            op("act", lambda e_, h=h: e_.activation(out=ag[:, 0:NB], in_=rg[:, 0:NB], func=AF.Exp,
                                                   scale=clam[:, e, h:h + 1]),
               reads=["rg", "clam"], writes=["ag"])
            op("dve", lambda e_: e_.tensor_tensor(out=mg[:, 0:NB], in0=ag[:, 0:NB], in1=ag[:, 0:NB], op=ALU.mult),
               reads=["ag"], writes=["mg"])
            op("act", lambda e_: e_.activation(out=mg[:, 0:NB], in_=mg[:, 0:NB], func=AF.Sqrt, scale=-1.0,
                                              bias=ones_c[:, 0:1]),
               reads=["mg", "ones"], writes=["mg"])
            if first:
                op("dve", lambda e_: e_.memset(ag[:, 0:1], 0.0), reads=["ag"], writes=["ag"])
                op("dve", lambda e_: e_.memset(mg[:, 0:1], 1.0), reads=["mg"], writes=["mg"])
            op("dve", lambda e_: e_.tensor_tensor(out=ig[:, 0:NB], in0=ig[:, 0:NB], in1=mg[:, 0:NB], op=ALU.mult),
               reads=["ig", "mg"], writes=["ig"])
            op("dve", lambda e_: e_.tensor_tensor(out=ig[:, 0:NB], in0=ig[:, 0:NB], in1=cv[:, 0:NB], op=ALU.mult),
               reads=["ig", "cv"], writes=["ig"])
            for s in range(nseg):
                op("dve", lambda e_, s=s, h=h: e_.tensor_tensor_scan(out=h3[:, s, :], data0=a3[:, s, :], data1=i3[:, s, :],
                                                                    initial=hst(h)[:, s:s + 1], op0=ALU.mult, op1=ALU.add),
                   reads=["ag", "ig", hk], writes=["hs"])
            op("act", lambda e_, h=h: e_.activation(out=hst(h), in_=h3[:, :, L - 1], func=AF.Copy),
               reads=["hs"], writes=[hk])
            wg = load_w(w_in_even[e], 16, (16 + h) * 128)
            bg = proj(wg, 16, zrhs, NB, ["xnT"])
            gelu_tanh(pbank[bg][:, 0:NB], [("pb", bg)], catT[:, 8 + h, 0:NB], ["catT"], NB,
                      mul_in=hs[:, 0:NB], mkeys=["hs"])
        for dc in range(16):
            wo = load_w(w_out_even[e], 16, dc * 128)
            by = proj(wo, 16, lambda k: catT[:, k, 0:NB], NB, ["catT"])
            op("act", lambda e_, by=by: e_.activation(out=tmpb[:, 0:NB], in_=pbank[by][:, 0:NB], func=AF.Copy),
               reads=[("pb", by)], writes=["tmpb"])
            add_T_into_x(tmpb, "tmpb", TB, dc)
        if samp:
            rows_out(lambda ct: poolbuf_s[:, ct, 0:8, :].rearrange("p b r -> p (b r)"), 120, 8, o_pool_s[e, 0:120, :], [pk])
            rows_out(lambda ct: poolbuf_s[:, ct, 8:16, :].rearrange("p b r -> p (b r)"), 120, 8, o_pool_s[e, 120:240, :], [pk])
            rows_out(lambda ct: convbuf_s[:, ct, :, :].rearrange("p b r -> p (b r)"), 48, 8, o_conv_s[e], [ck])
            rows_out(lambda ct: hst_s[:, ct, :], 16, 8, o_h_s[e], [hk])
        elif last:
            rows_out(lambda ct: poolbuf_p[:, e, ct, :], 15, 8, o_pool_p[e], [pk])
            rows_out(lambda ct: convbuf_p[:, e, ct, :], 3, 8, o_conv_p[e], [ck])
            rows_out(lambda ct: hst_p[:, e, ct:ct + 1], 1, 8, o_h_p[e], [hk])

    if do_mix and odds:
        Fsel = sb("Fsel", [128, 8, 8, 128], BF16)
        mask_ts = sb("mask_ts", [128, 128])
        AB8 = sb("AB8", [64, 2, 2, 128])
        Xcar = sb("Xcar", [64, 2, 2, 128])
        dcol = sb("dcol", [128, 2, 16])
        Xs_in = x[:].rearrange("p a b -> p (a b)")[0:64, 2048:6144].rearrange("p (r g b) -> p r g b", g=128, b=NSAMP)
        Xs_out = Xs_in

    def sincos(turns, s_out, c_out, shape, keys):
        n = 1
        for v in shape[1:]:
            n *= v
        ti = tmpa[0:shape[0], 0:n].bitcast(I32)
        tf = tmpb[0:shape[0], 0:n]
        tm = tmpb[0:shape[0], n:2 * n]
        for (shift, dst) in ((0.0, s_out), (0.25, c_out)):
            flat = lambda ap: ap
            op("dve", lambda e, shift=shift: e.tensor_scalar(out=tf, in0=turns, scalar1=shift, scalar2=None, op0=ALU.add),
               reads=keys, writes=["tmpb"])
            op("dve", lambda e: e.tensor_copy(out=ti, in_=tf), reads=["tmpb"], writes=["tmpa"])
            op("dve", lambda e: e.tensor_copy(out=tm, in_=ti), reads=["tmpa"], writes=["tmpb"])
            op("dve", lambda e: e.tensor_tensor(out=tf, in0=tf, in1=tm, op=ALU.subtract), reads=["tmpb"], writes=["tmpb"])
            op("dve", lambda e: e.tensor_scalar(out=tm, in0=tf, scalar1=0.5, scalar2=None, op0=ALU.is_gt),
               reads=["tmpb"], writes=["tmpb"])
            op("dve", lambda e: e.tensor_tensor(out=tf, in0=tf, in1=tm, op=ALU.subtract), reads=["tmpb"], writes=["tmpb"])
            op("dve", lambda e: e.tensor_scalar(out=tm, in0=tf, scalar1=-0.5, scalar2=None, op0=ALU.is_lt),
               reads=["tmpb"], writes=["tmpb"])
            op("dve", lambda e: e.tensor_tensor(out=tf, in0=tf, in1=tm, op=ALU.add), reads=["tmpb"], writes=["tmpb"])
            op("act", lambda e, dst=dst: e.activation(out=dst, in_=tf, func=AF.Sin, scale=TWO_PI),
               reads=["tmpb"], writes=keys)

    def s5_consts():
        onesb = tmpa[:, 0:16].bitcast(BF16)[:, 0:16]
        op("pool", lambda e: e.memset(Fsel[:], 0.0), writes=["Fsel"])
        op("pool", lambda e: e.memset(onesb, 1.0), writes=["onesb"])
        for a_ in range(8):
            for b_ in range(8):
                op("pool", lambda e, a_=a_, b_=b_: e.affine_select(
                    out=Fsel[:, a_, b_, 16 * b_:16 * b_ + 16], in_=onesb, pattern=[[-1, 16]],
                    compare_op=ALU.is_equal, fill=0.0, base=-16 * a_, channel_multiplier=1),
                   reads=["onesb"], writes=["Fsel"])
        op("pool", lambda e: e.memset(mask_ts[:], 1.0), writes=["mask_ts"])
        op("pool", lambda e: e.affine_select(out=mask_ts[:].rearrange("p (t c) -> p t c", c=16),
                                            in_=mask_ts[:].rearrange("p (t c) -> p t c", c=16),
                                            pattern=[[16, 8], [0, 16]], compare_op=ALU.is_ge, fill=0.0, base=15,
                                            channel_multiplier=-1), reads=["mask_ts"], writes=["mask_ts"])
        for o in range(2):
            dma("sp", dcol[:, o, :], ssm_d[o].rearrange("(k p) -> p k", p=128), writes=["dcol"],
                allow_slow_non_contiguous=True)
        op("dve", lambda e: e.memset(Xcar[:], 0.0), writes=["Xcar"])

    def s5_setup(o):
        off = 0
        G = 128
        GB = 16

        def fb(words, parts=64):
            nonlocal off
            v = arena[0:parts, off:off + words]
            off += words
            return v
        lr, li, dt_, lrdt, turns, mag, abr, abi, den, nr, fre, fim, t0, t1 = (fb(G) for _ in range(14))
        g16 = lambda v: v.rearrange("p (g c) -> p g c", c=16)
        braw, biraw = fb(G * 16), fb(G * 16)
        brt, bit = g16(braw), g16(biraw)
        Bbr, Bbi, Crt, Cit = (g16(fb(G * 16)) for _ in range(4))
        PW = [[fb(8 * G).rearrange("p (s g) -> p s g", g=G) for _ in range(2)] for _ in range(3)]
        craw = arena[:, off:off + 16 * 64].rearrange("p (t q) -> p t q", q=64)
        off += 16 * 64
        assert off <= ARENA_F
        gm = lambda v: v.rearrange("p (g m) -> p g m", m=128)
        Lt = [gm(braw), gm(biraw)]
        xflat = x[:].rearrange("p a b -> p (a b)")
        Rt = [gm(xflat[0:64, 0:2048]), gm(xflat[0:64, 2048:4096])]
        CAt = xflat[0:64, 4096:6144].bitcast(BF16).rearrange("p (g m) -> p g m", m=256)
        Msb = gm(xflat[:, 6144:7168].bitcast(BF16))
        WTsb = gm(xflat[:, 7168:8192].bitcast(BF16))
        scr = xnT[:].rearrange("p a b -> p (a b)").bitcast(F32)[0:64, :]
        K = ["s5s"]
        slow = dict(allow_slow_non_contiguous=True)
        dma("sp", lr, lam_re[o].rearrange("g p -> p g"), writes=K, **slow)
        dma("sp", li, lam_im[o].rearrange("g p -> p g"), writes=K, **slow)
        dma("sp", dt_, log_step[o:o + 1, :].broadcast_to([64, 128]), writes=K)
        dma("sp", brt, b_re[o].rearrange("g p c -> p g c"), writes=K)
        dma("sp", bit, b_im[o].rearrange("g p c -> p g c"), writes=K)
        for (csrc, cdst) in ((c_re, Crt), (c_im, Cit)):
            dma("sp", craw, csrc[o].rearrange("(t r) q -> r t q", r=128), writes=["craw"])
            for t in range(16):
                b = next_bank()
                op("pe", lambda e, t=t, b=b: e.transpose(out=pbank[b][0:64, 0:128], in_=craw[:, t, :], identity=ident_f[:]),
                   reads=["craw", "identf"], writes=[("pb", b)])
                op("act", lambda e, t=t, b=b, cdst=cdst: e.activation(
                    out=cdst[:, t * 8:(t + 1) * 8, :], in_=pbank[b][0:64, 0:128].rearrange("p (g c) -> p g c", c=16),
                    func=AF.Copy), reads=[("pb", b)], writes=K)
        A = lambda f: op("act", f, reads=K, writes=K)
        V = lambda f: op("dve", f, reads=K, writes=K)
        A(lambda e: e.activation(out=dt_, in_=dt_, func=AF.Exp))
        V(lambda e: e.tensor_scalar(out=lr, in0=lr, scalar1=-1e-4, scalar2=None, op0=ALU.min))
        V(lambda e: e.tensor_tensor(out=lrdt, in0=lr, in1=dt_, op=ALU.mult))
        V(lambda e: e.tensor_tensor(out=turns, in0=li, in1=dt_, op=ALU.mult))
        V(lambda e: e.tensor_scalar(out=turns, in0=turns, scalar1=1.0 / TWO_PI, scalar2=None, op0=ALU.mult))
        V(lambda e: e.tensor_scalar(out=mag, in0=lrdt, scalar1=1.0 / 720.0, scalar2=None, op0=ALU.mult))
        for cf in (1.0 / 120.0, 1.0 / 24.0, 1.0 / 6.0, 0.5, 1.0):
            V(lambda e, cf=cf: e.scalar_tensor_tensor(out=mag, in0=mag, scalar=cf, in1=lrdt, op0=ALU.add, op1=ALU.mult))
        V(lambda e: e.tensor_scalar(out=mag, in0=mag, scalar1=1.0, scalar2=None, op0=ALU.add))
        ti_ = tmpa[0:64, 0:G].bitcast(I32)
        V(lambda e: e.tensor_copy(out=ti_, in_=turns))
        V(lambda e: e.tensor_copy(out=t0, in_=ti_))
        V(lambda e: e.tensor_tensor(out=t0, in0=turns, in1=t0, op=ALU.subtract))
        V(lambda e: e.tensor_scalar(out=t1, in0=t0, scalar1=0.5, scalar2=None, op0=ALU.is_gt))
        V(lambda e: e.tensor_tensor(out=t0, in0=t0, in1=t1, op=ALU.subtract))
        V(lambda e: e.tensor_scalar(out=t1, in0=t0, scalar1=-0.5, scalar2=None, op0=ALU.is_lt))
        V(lambda e: e.tensor_tensor(out=t0, in0=t0, in1=t1, op=ALU.add))
        V(lambda e: e.tensor_scalar(out=t0, in0=t0, scalar1=TWO_PI, scalar2=None, op0=ALU.mult))
        V(lambda e: e.tensor_tensor(out=t1, in0=t0, in1=t0, op=ALU.mult))
        fact = math.factorial
        V(lambda e: e.tensor_scalar(out=abi, in0=t1, scalar1=-1.0 / fact(15), scalar2=None, op0=ALU.mult))
        for kk in (6, 5, 4, 3, 2, 1):
            cf = ((-1.0) ** kk) / fact(2 * kk + 1)
            V(lambda e, cf=cf: e.scalar_tensor_tensor(out=abi, in0=abi, scalar=cf, in1=t1, op0=ALU.add, op1=ALU.mult))
        V(lambda e: e.scalar_tensor_tensor(out=abi, in0=abi, scalar=1.0, in1=t0, op0=ALU.add, op1=ALU.mult))
        V(lambda e: e.tensor_scalar(out=abr, in0=t1, scalar1=1.0 / fact(16), scalar2=None, op0=ALU.mult))
        for kk in (7, 6, 5, 4, 3, 2, 1):
            cf = ((-1.0) ** kk) / fact(2 * kk)
            V(lambda e, cf=cf: e.scalar_tensor_tensor(out=abr, in0=abr, scalar=cf, in1=t1, op0=ALU.add, op1=ALU.mult))
        V(lambda e: e.tensor_scalar(out=abr, in0=abr, scalar1=1.0, scalar2=None, op0=ALU.add))
        V(lambda e: e.tensor_tensor(out=abr, in0=abr, in1=mag, op=ALU.mult))
        V(lambda e: e.tensor_tensor(out=abi, in0=abi, in1=mag, op=ALU.mult))
        V(lambda e: e.tensor_tensor(out=den, in0=lr, in1=lr, op=ALU.mult))
        V(lambda e: e.tensor_tensor(out=t0, in0=li, in1=li, op=ALU.mult))
        V(lambda e: e.tensor_tensor(out=den, in0=den, in1=t0, op=ALU.add))
        V(lambda e: e.reciprocal(out=den, in_=den))
        V(lambda e: e.tensor_scalar(out=nr, in0=abr, scalar1=-1.0, scalar2=None, op0=ALU.add))
        V(lambda e: e.tensor_tensor(out=t0, in0=nr, in1=lr, op=ALU.mult))
        V(lambda e: e.tensor_tensor(out=t1, in0=abi, in1=li, op=ALU.mult))
        V(lambda e: e.tensor_tensor(out=fre, in0=t0, in1=t1, op=ALU.add))
        V(lambda e: e.tensor_tensor(out=fre, in0=fre, in1=den, op=ALU.mult))
        V(lambda e: e.tensor_tensor(out=t0, in0=abi, in1=lr, op=ALU.mult))
        V(lambda e: e.tensor_tensor(out=t1, in0=nr, in1=li, op=ALU.mult))
        V(lambda e: e.tensor_tensor(out=fim, in0=t0, in1=t1, op=ALU.subtract))
        V(lambda e: e.tensor_tensor(out=fim, in0=fim, in1=den, op=ALU.mult))

        def cmul(ore, oim, ar, ai, br, bi, tmp, neg_im=False):
            V(lambda e: e.tensor_tensor(out=ore, in0=ar, in1=br, op=ALU.mult))
            V(lambda e: e.tensor_tensor(out=tmp, in0=ai, in1=bi, op=ALU.mult))
            V(lambda e: e.tensor_tensor(out=ore, in0=ore, in1=tmp, op=ALU.subtract))
            V(lambda e: e.tensor_tensor(out=oim, in0=ar, in1=bi, op=ALU.mult))
            V(lambda e: e.tensor_tensor(out=tmp, in0=ai, in1=br, op=ALU.mult))
            V(lambda e: e.tensor_tensor(out=oim, in0=oim, in1=tmp, op=ALU.add))
            if neg_im:
                V(lambda e: e.tensor_scalar(out=oim, in0=oim, scalar1=-1.0, scalar2=None, op0=ALU.mult))

        bc16 = lambda v: v.unsqueeze(2).broadcast_to([64, G, 16])
        cmul(Bbr, Bbi, bc16(fre), bc16(fim), brt, bit, g16(scr[:, 0:2048]))
        PL, PR, PC = PW
        V(lambda e: e.tensor_copy(out=PC[0][:, 0, :], in_=abr))
        V(lambda e: e.tensor_copy(out=PC[1][:, 0, :], in_=abi))
        for t_ in range(1, 8):
            cmul(PC[0][:, t_, :], PC[1][:, t_, :], PC[0][:, t_ - 1, :], PC[1][:, t_ - 1, :], abr, abi, t0)
        for ri in range(2):
            V(lambda e, ri=ri: e.memset(PL[ri][:, 7, :], 1.0 - ri))
            V(lambda e, ri=ri: e.memset(PR[ri][:, 7, :], 1.0 - ri))
            for s_ in range(7):
                V(lambda e, ri=ri, s_=s_: e.tensor_copy(out=PL[ri][:, s_, :], in_=PC[ri][:, 6 - s_, :]))
        for t_ in range(7):
            pr_, pi_ = PC[0][:, 6 - t_, :], PC[1][:, 6 - t_, :]
            V(lambda e, pr_=pr_: e.tensor_tensor(out=t0, in0=pr_, in1=pr_, op=ALU.mult))
            V(lambda e, pi_=pi_: e.tensor_tensor(out=t1, in0=pi_, in1=pi_, op=ALU.mult))
            V(lambda e: e.tensor_tensor(out=t0, in0=t0, in1=t1, op=ALU.add))
            V(lambda e: e.reciprocal(out=t0, in_=t0))
            V(lambda e, pr_=pr_, t_=t_: e.tensor_tensor(out=PR[0][:, t_, :], in0=pr_, in1=t0, op=ALU.mult))
            V(lambda e, pi_=pi_, t_=t_: e.scalar_tensor_tensor(out=PR[1][:, t_, :], in0=pi_, scalar=-1.0, in1=t0,
                                                              op0=ALU.mult, op1=ALU.mult))
        op("dve", lambda e: e.tensor_copy(out=AB8[:, o, 0, :], in_=PW[2][0][:, 7, :]), reads=K, writes=["AB8"])
        op("dve", lambda e: e.tensor_copy(out=AB8[:, o, 1, :], in_=PW[2][1][:, 7, :]), reads=K, writes=["AB8"])

        v4 = lambda v: v.rearrange("p g (s c) -> p g s c", c=16)
        tmp4 = scr[:, 0:GB * 128].rearrange("p (g s c) -> p g s c", s=8, c=16)
        for g0 in range(0, G, GB):
            gs = slice(g0, g0 + GB)
            pwv = lambda ti, ri: PW[ti][ri][:, :, gs].rearrange("p s g -> p g s").unsqueeze(3).broadcast_to([64, GB, 8, 16])
            bbv = lambda v: v[:, gs, :].unsqueeze(2).broadcast_to([64, GB, 8, 16])
            cmul(v4(Lt[0]), v4(Lt[1]), pwv(0, 0), pwv(0, 1), bbv(Bbr), bbv(Bbi), tmp4)
            cmul(v4(Rt[0]), v4(Rt[1]), pwv(1, 0), pwv(1, 1), bbv(Crt), bbv(Cit), tmp4, neg_im=True)
            for gl in range(GB):
                b = next_bank()
                op("pe", lambda e, gl=gl, b=b: e.matmul(pbank[b][:, 0:128], lhsT=Lt[0][:, gl, :], rhs=Rt[0][:, gl, :],
                                                       start=True, stop=False), reads=K, writes=[("pb", b)])
                op("pe", lambda e, gl=gl, b=b: e.matmul(pbank[b][:, 0:128], lhsT=Lt[1][:, gl, :], rhs=Rt[1][:, gl, :],
                                                       start=False, stop=True), reads=K, writes=[("pb", b)])
                op("dve", lambda e, gl=gl, b=b: e.tensor_tensor(out=Msb[:, gl, :], in0=pbank[b][:, 0:128], in1=mask_ts[:],
                                                               op=ALU.mult), reads=[("pb", b), "mask_ts"], writes=["Msb"])
                b2 = next_bank()
                for ri in range(2):
                    op("pe", lambda e, gl=gl, b2=b2, ri=ri: e.transpose(out=pbank[b2][:, ri * 64:(ri + 1) * 64],
                                                                       in_=Lt[ri][:, gl, :], identity=ident_f[0:64, 0:64]),
                       reads=K + ["identf"], writes=[("pb", b2)])
                op("act", lambda e, gl=gl, b2=b2: e.activation(out=WTsb[:, gl, :], in_=pbank[b2][:, 0:128], func=AF.Copy),
                   reads=[("pb", b2)], writes=["WTsb"])
            cmul(v4(Rt[0]), v4(Rt[1]), pwv(2, 0), pwv(2, 1), bbv(Crt), bbv(Cit), tmp4, neg_im=True)
            for ri in range(2):
                op("act", lambda e, ri=ri: e.activation(out=CAt[:, :, ri * 128:(ri + 1) * 128], in_=Rt[ri][:], func=AF.Copy),
                   reads=K, writes=["CAt"])
            dma("sp", M_all[o][:, gs, :], Msb[:], reads=["Msb"], writes=[("Mall", o)])
            dma("sp", WT_all[o][:, gs, :], WTsb[:], reads=["WTsb"], writes=[("WTall", o)])
            dma("sp", CA_all[o][:, gs, :], CAt[:], reads=["CAt"], writes=[("CAall", o)])

    def s5_mixer(kind, bidx, l):
        o = l // 2
        samp = kind == "s"
        TB = 1 if samp else 4
        NB = TB * 128
        NJ = NB // 8
        last = (not samp) and bidx == cfg["last_p"]
        QG = 16
        off = 0

        def cv_(words, dt=F32, parts=128):
            nonlocal off
            v = arena[0:parts, off:off + words]
            off += words
            return v.bitcast(BF16) if dt == BF16 else v
        gyT = cv_(4096, BF16).rearrange("p (c n) -> p c n", n=512)
        Msb = cv_(QG * 64, BF16).rearrange("p (g m) -> p g m", m=128)
        WTsb = cv_(QG * 64, BF16).rearrange("p (g m) -> p g m", m=128)
        CAsb = cv_(QG * 128, BF16, 64).rearrange("p (g m) -> p g m", m=256)
        uTsb = cv_(QG * 32, BF16).rearrange("p (g j) -> p g j", j=64)
        Xh = cv_(2 * QG * 65, F32, 64).rearrange("p (r g j) -> p r g j", g=QG, j=65)
        Xb = cv_(QG * 64, BF16, 64).rearrange("p (r g j) -> p r g j", g=QG, j=64)
        Ysb = cv_(QG * 32, BF16).rearrange("p (g j) -> p g j", j=64)
        s1 = cv_(2 * QG * 16, F32, 64).rearrange("p (r g j) -> p r g j", g=QG, j=16)
        s2 = cv_(2 * QG * 16, F32, 64).rearrange("p (r g j) -> p r g j", g=QG, j=16)
        yt = cv_(512)
        assert off <= ARENA_F - 1024
        rb = arena[:, ARENA_F - 1024:ARENA_F]

        if samp:
            for (src, ri) in ((st_re, 0), (st_im, 1)):
                dma("sp", rb.rearrange("p (b q) -> p b q", q=64), src[o].rearrange("b g p -> g b p"), writes=["rowbuf"])
                for b_ in range(NSAMP):
                    f = st["pf"]
                    st["pf"] = 1 - f
                    pview = ptr_f[f][:].rearrange("p a b -> p (a b)")
                    op("pe", lambda e, b_=b_, pview=pview: e.transpose(out=pview[0:64, 0:128], in_=rb[:, b_ * 64:(b_ + 1) * 64],
                                                                     identity=ident_f[:]),
                       reads=["rowbuf", "identf"], writes=[("ptr_f", f)])
                    op("act", lambda e, b_=b_, ri=ri, pview=pview: e.activation(out=Xs_in[:, ri, :, b_], in_=pview[0:64, 0:128],
                                                                               func=AF.Copy),
                       reads=[("ptr_f", f)], writes=["Xs_in"])

        rms_to_T(TB, 4 + l)
        for qd in range(128 // QG):
            g0 = qd * QG
            gsl = slice(g0, g0 + QG)
            dma("sp", Msb[:], M_all[o][:, gsl, :], reads=[("Mall", o)], writes=["Msb"])
            dma("sp", WTsb[:], WT_all[o][:, gsl, :], reads=[("WTall", o)], writes=["WTsb"])
            dma("sp", CAsb[:], CA_all[o][:, gsl, :], reads=[("CAall", o)], writes=["CAsb"])
            for gl in range(QG):
                g = g0 + gl
                kc, q = g // 8, g % 8
                b = next_bank()
                for s_ in range(8):
                    op("pe", lambda e, b=b, q=q, s_=s_, kc=kc: e.matmul(pbank[b][:, 0:NJ], lhsT=Fsel[:, q, s_, :],
                                                                      rhs=xnT[:, kc, s_:NB:8], start=(s_ == 0), stop=(s_ == 7)),
                       reads=["Fsel", "xnT"], writes=[("pb", b)])
                op("act", lambda e, b=b, gl=gl: e.activation(out=uTsb[:, gl, 0:NJ], in_=pbank[b][:, 0:NJ], func=AF.Copy),
                   reads=[("pb", b)], writes=["uTsb"])
                b2 = next_bank()
                for ri in range(2):
                    op("pe", lambda e, b2=b2, gl=gl, ri=ri: e.matmul(pbank[b2][0:64, ri * 64:ri * 64 + NJ],
                                                                    lhsT=WTsb[:, gl, ri * 64:(ri + 1) * 64],
                                                                    rhs=uTsb[:, gl, 0:NJ], start=True, stop=True),
                       reads=["WTsb", "uTsb"], writes=[("pb", b2)])
                op("dve", lambda e, b2=b2, gl=gl: e.tensor_copy(
                    out=Xh[:, :, gl, 1:NJ + 1], in_=pbank[b2][0:64, 0:128].rearrange("p (r j) -> p r j", j=64)[:, :, 0:NJ]),
                   reads=[("pb", b2)], writes=["Xh"])
            abr_ = AB8[:, o, 0, gsl]
            abi_ = AB8[:, o, 1, gsl]
            C = dict(chain=True)
            if not samp:
                op("dve", lambda e, gsl=gsl: e.tensor_copy(out=Xh[:, :, :, 0], in_=Xcar[:, o, :, gsl]), reads=["Xcar", "Xh"], writes=["Xh"])
                t1 = s1[:, :, :, 0]
                t2 = s2[:, :, :, 0]
                for j in range(NJ):
                    prev, cur = Xh[:, :, :, j], Xh[:, :, :, j + 1]
                    op("dve", lambda e, prev=prev, abr_=abr_: e.tensor_tensor(out=t1, in0=prev, in1=abr_.unsqueeze(1).broadcast_to([64, 2, QG]),
                                                                  op=ALU.mult), reads=["Xh", "AB8"], writes=["s1"], **C)
                    op("dve", lambda e, cur=cur: e.tensor_tensor(out=t1, in0=t1, in1=cur, op=ALU.add),
                       reads=["Xh", "s1"], writes=["s1"], **C)
                    op("dve", lambda e, prev=prev, abi_=abi_: e.tensor_tensor(out=t2[:, 0, :], in0=prev[:, 1, :], in1=abi_, op=ALU.mult),
                       reads=["Xh", "AB8"], writes=["s2"], **C)
                    op("dve", lambda e, prev=prev, abi_=abi_: e.tensor_tensor(out=t2[:, 1, :], in0=prev[:, 0, :], in1=abi_, op=ALU.mult),
                       reads=["Xh", "AB8"], writes=["s2"], **C)
                    op("dve", lambda e, cur=cur: e.tensor_tensor(out=cur[:, 0, :], in0=t1[:, 0, :], in1=t2[:, 0, :],
                                                                op=ALU.subtract), reads=["s1", "s2"], writes=["Xh"], **C)
                    op("dve", lambda e, cur=cur: e.tensor_tensor(out=cur[:, 1, :], in0=t1[:, 1, :], in1=t2[:, 1, :],
                                                                op=ALU.add), reads=["s1", "s2"], writes=["Xh"], **C)
                op("dve", lambda e, gsl=gsl: e.tensor_copy(out=Xcar[:, o, :, gsl], in_=Xh[:, :, :, NJ]), reads=["Xh"], writes=["Xcar"], **C)
                op("act", lambda e: e.activation(out=Xb[:, :, :, 0:NJ], in_=Xh[:, :, :, 0:NJ], func=AF.Copy),
                   reads=["Xh"], writes=["Xb"])
            else:
                xin = Xs_in[:, :, gsl, :]
                w_ = Xh[:, :, :, 1:NJ + 1]
                bc = lambda v: v.unsqueeze(2).broadcast_to([64, QG, NSAMP])
                abr4 = abr_.unsqueeze(1).unsqueeze(3).broadcast_to([64, 2, QG, NSAMP])
                abi3 = abi_.unsqueeze(2).broadcast_to([64, QG, NSAMP])
                xo = Xs_out[:, :, gsl, :]
                op("dve", lambda e, xin=xin, abr4=abr4: e.tensor_tensor(out=s1[:], in0=xin, in1=abr4, op=ALU.mult),
                   reads=["Xs_in", "AB8"], writes=["s1"])
                op("dve", lambda e: e.tensor_tensor(out=s1[:], in0=s1[:], in1=w_, op=ALU.add), reads=["s1", "Xh"], writes=["s1"])
                op("dve", lambda e, xin=xin, abi3=abi3: e.tensor_tensor(out=s2[:, 0], in0=xin[:, 1], in1=abi3, op=ALU.mult),
                   reads=["Xs_in", "AB8"], writes=["s2"])
                op("dve", lambda e, xin=xin, abi3=abi3: e.tensor_tensor(out=s2[:, 1], in0=xin[:, 0], in1=abi3, op=ALU.mult),
                   reads=["Xs_in", "AB8"], writes=["s2"])
                op("act", lambda e, xin=xin: e.activation(out=Xb[:, :, :, 0:NJ], in_=xin, func=AF.Copy), reads=["Xs_in"], writes=["Xb"])
                op("dve", lambda e, xo=xo: e.tensor_tensor(out=xo[:, 0], in0=s1[:, 0], in1=s2[:, 0], op=ALU.subtract),
                   reads=["s1", "s2"], writes=["Xs_in"])
                op("dve", lambda e, xo=xo: e.tensor_tensor(out=xo[:, 1], in0=s1[:, 1], in1=s2[:, 1], op=ALU.add),
                   reads=["s1", "s2"], writes=["Xs_in"])
            for gl in range(QG):
                b = next_bank()
                op("pe", lambda e, b=b, gl=gl: e.matmul(pbank[b][:, 0:NJ], lhsT=Msb[:, gl, :], rhs=uTsb[:, gl, 0:NJ],
                                                       start=True, stop=False), reads=["Msb", "uTsb"], writes=[("pb", b)])
                for ri in range(2):
                    op("pe", lambda e, b=b, gl=gl, ri=ri: e.matmul(pbank[b][:, 0:NJ], lhsT=CAsb[:, gl, ri * 128:(ri + 1) * 128],
                                                                  rhs=Xb[:, ri, gl, 0:NJ], start=False, stop=(ri == 1)),
                       reads=["CAsb", "Xb"], writes=[("pb", b)])
                op("act", lambda e, b=b, gl=gl: e.activation(out=Ysb[:, gl, 0:NJ], in_=pbank[b][:, 0:NJ], func=AF.Copy),
                   reads=[("pb", b)], writes=["Ysb"])
            for kl in range(QG // 8):
                kc = qd * (QG // 8) + kl
                b = next_bank()
                for q in range(8):
                    for t in range(8):
                        op("pe", lambda e, b=b, q=q, t=t, kl=kl: e.matmul(pbank[b][:, t:NB:8], lhsT=Fsel[:, t, q, :],
                                                                         rhs=Ysb[:, kl * 8 + q, 0:NJ],
                                                                         start=(q == 0 and t == 0), stop=(q == 7 and t == 7)),
                           reads=["Fsel", "Ysb"], writes=[("pb", b)])
                op("dve", lambda e, b=b, kc=kc: e.scalar_tensor_tensor(out=yt[:, 0:NB], in0=xnT[:, kc, 0:NB],
                                                                      scalar=dcol[:, o, kc:kc + 1], in1=pbank[b][:, 0:NB],
                                                                      op0=ALU.mult, op1=ALU.add),
                   reads=["xnT", "dcol", ("pb", b)], writes=["yt"])
                gelu_tanh(yt[:, 0:NB], ["yt"], gyT[:, kc, 0:NB], ["gyT"], NB)
        for c in range(16):
            w1 = load_w(w_glu[o], 16, c * 128)
            b1 = proj(w1, 16, lambda k: gyT[:, k, 0:NB], NB, ["gyT"])
            w2 = load_w(w_glu[o], 16, D + c * 128)
            b2 = proj(w2, 16, lambda k: gyT[:, k, 0:NB], NB, ["gyT"])
            op("act", lambda e, b2=b2: e.activation(out=tmpa[:, 0:NB], in_=pbank[b2][:, 0:NB], func=AF.Sigmoid),
               reads=[("pb", b2)], writes=["tmpa"])
            op("dve", lambda e, b1=b1: e.tensor_tensor(out=tmpb[:, 0:NB], in0=pbank[b1][:, 0:NB], in1=tmpa[:, 0:NB], op=ALU.mult),
               reads=[("pb", b1), "tmpa"], writes=["tmpb"])
            add_T_into_x(tmpb, "tmpb", TB, c)
        if samp or last:
            for ri, (dp, ds) in enumerate(((o_re_p, o_re_s), (o_im_p, o_im_s))):
                nb = NSAMP if samp else 1
                for b_ in range(nb):
                    f = st["pf"]
                    st["pf"] = 1 - f
                    pview = ptr_f[f][:].rearrange("p a b -> p (a b)")
                    src = Xs_out[:, ri, :, b_] if samp else Xcar[:, o, ri, :]
                    op("pe", lambda e, src=src, pview=pview: e.transpose(out=pview[:, 0:64], in_=src, identity=ident_f[0:64, 0:64]),
                       reads=["Xs_in", "Xcar", "identf"], writes=[("ptr_f", f)])
                    op("act", lambda e, b_=b_, pview=pview: e.activation(out=rb[:, b_ * 64:(b_ + 1) * 64], in_=pview[:, 0:64],
                                                                       func=AF.Copy), reads=[("ptr_f", f)], writes=["rowbuf"])
                if samp:
                    t = dma("sp", ds[o].rearrange("b g p -> g b p"), rb.rearrange("p (b q) -> p b q", q=64), reads=["rowbuf"])
                else:
                    t = dma("sp", dp[o], rb[:, 0:64], reads=["rowbuf"])
                P.out_tokens.append(t)

    op("dve", lambda e: e.memset(stat[:, 3:4], EPS), writes=["epsb"])
    op("pool", lambda e: e.memset(ident_f[:], 1.0), writes=["identf"])
    op("pool", lambda e: e.affine_select(out=ident_f[:], in_=ident_f[:], pattern=[[-1, 128]],
                                        compare_op=ALU.is_equal, fill=0.0, base=0, channel_multiplier=1),
       reads=["identf"], writes=["identf"])
    op("dve", lambda e: e.tensor_copy(out=ident_b[:], in_=ident_f[:]), reads=["identf"], writes=["identb"])
    for g in range(13):
        dma("sp", gcol[:, g, :], norms[g].rearrange("(k p) -> p k", p=128), writes=["gcol"],
            allow_slow_non_contiguous=True)
    if do_mix and evens:
        op("dve", lambda e: e.memset(ones_c[:], 1.0), writes=["ones"])
        even_setup()
    P.barrier()
    if do_mix and odds:
        s5_consts()
        P.barrier()
        for l in odds:
            s5_setup(l // 2)
            P.barrier()

    for (kind, bidx) in cfg["blocks"]:
        if kind == "p":
            xin, yo, TB = xp[bidx * 512:(bidx + 1) * 512, :], yp[bidx * 512:(bidx + 1) * 512, :], 4
        else:
            xin, yo, TB = xs_in, ys, 1
        for i in range(TB):
            dma("sp", x[:, i, :], xin[i * 128:(i + 1) * 128, :], writes=[("x", i)])
        for l in layers:
            if do_ffn:
                ffn(TB, l, w1i[l], w1o[l])
                P.barrier()
            if do_mix:
                if l % 2 == 0:
                    even_mixer(kind, bidx, l)
                else:
                    s5_mixer(kind, bidx, l)
                P.barrier()
            if do_ffn:
                ffn(TB, 8 + l, w2i[l], w2o[l])
                P.barrier()
        final_norm_store(TB, yo, cfg["final"])
        P.barrier()

    P.emit("sp", "nop", extra=P.out_tokens)

    def sem_of(tok):
        kind, key, val = tok
        return (sems[key] if kind == "eng" else dsems[key]), val

    def replay(name, eng):
        for waits, fn, tok in P.ops[name]:
            for w in waits:
                s, v = sem_of(w)
                eng.wait_ge(s, v)
            if fn == "nop":
                continue
            s, v = sem_of(tok)
            ins = fn(eng)
            ins.then_inc(s, 16 if tok[0] == "dma" else 1)

    with nc.Block() as block:
        @block.tensor
        def _(e):
            replay("pe", e)

        @block.scalar
        def _(e):
            replay("act", e)

        @block.vector
        def _(e):
            replay("dve", e)

        @block.gpsimd
        def _(e):
            replay("pool", e)

        @block.sync
        def _(e):
            replay("sp", e)

    es.close()
    return nc, in_names


_NC_CACHE = {}


def make_in_maps(inp, in_names):
    f = lambda a: np.ascontiguousarray(np.asarray(a, dtype=np.float32))
    norms = np.concatenate([f(inp["norm_ffn1"]), f(inp["norm_mix"]), f(inp["norm_ffn2"]),
                            f(inp["final_norm"])[None, :]], axis=0)
    xpr = f(inp["x_prompt"])
    xsa = f(inp["x_sample"]).reshape(128 * 8, D)
    evec = np.concatenate([f(inp["pool_scale"])[:, None, :], f(inp["conv_w"]), f(inp["conv_b"])[:, None, :],
                           f(inp["gate_a_b"])[:, None, :], f(inp["gate_x_b"])[:, None, :],
                           f(inp["rglru_lambda"])[:, None, :]], axis=1)
    shared = {"norms": norms, "evec": evec}
    for k in ("w_ffn1_in", "w_ffn1_out", "w_ffn2_in", "w_ffn2_out", "w_in_even", "pool_w", "gate_a_w", "gate_x_w",
              "w_out_even", "ssm_lambda_re", "ssm_lambda_im", "ssm_log_step", "ssm_b_re", "ssm_b_im", "ssm_d", "w_glu"):
        if k in in_names:
            shared[k] = f(inp[k])
    if "ssm_c_re" in in_names:
        shared["ssm_c_re"] = f(inp["ssm_c_re"]).reshape(2, 2048, 64)
        shared["ssm_c_im"] = f(inp["ssm_c_im"]).reshape(2, 2048, 64)
    in_maps = []
    for c in range(NCORES):
        m = {}
        sl = slice(c * NSAMP, (c + 1) * NSAMP)
        per = {"xp": xpr[c % 4], "xs": xsa[c * 128:(c + 1) * 128]}
        if "st_pool" in in_names:
            per["st_pool"] = f(inp["state_pool"])[:, sl].reshape(2, NSAMP * 15, 1024)
            per["st_conv"] = f(inp["state_conv"])[:, sl].reshape(2, NSAMP * 3, 1024)
            per["st_h"] = f(inp["state_rglru"])[:, sl]
        if "st_re" in in_names:
            per["st_re"] = f(inp["state_ssm_re"])[:, sl]
            per["st_im"] = f(inp["state_ssm_im"])[:, sl]
        for k in in_names:
            m[k] = np.ascontiguousarray(per[k]) if k in per else shared[k]
        in_maps.append(m)
    return in_maps


def gather(r):
    y_prompt = np.stack([r[c]["yp"] for c in range(4)], 0)
    y_sample = np.concatenate([r[c]["ys"] for c in range(NCORES)], 0).reshape(128, 8, D)
    pp = np.stack([r[c]["o_pool_p"] for c in range(4)], 1)
    psm = np.concatenate([r[c]["o_pool_s"].reshape(2, NSAMP, 15, 1024) for c in range(NCORES)], 1)
    cp = np.stack([r[c]["o_conv_p"] for c in range(4)], 1)
    csm = np.concatenate([r[c]["o_conv_s"].reshape(2, NSAMP, 3, 1024) for c in range(NCORES)], 1)
    hp = np.stack([r[c]["o_h_p"].reshape(2, 1024) for c in range(4)], 1)
    hsm = np.concatenate([r[c]["o_h_s"] for c in range(NCORES)], 1)
    rp = np.stack([r[c]["o_re_p"] for c in range(4)], 1)
    rs = np.concatenate([r[c]["o_re_s"] for c in range(NCORES)], 1)
    ip = np.stack([r[c]["o_im_p"] for c in range(4)], 1)
    is_ = np.concatenate([r[c]["o_im_s"] for c in range(NCORES)], 1)
    return (y_prompt, y_sample, pp, psm, cp, csm, hp, hsm, rp, rs, ip, is_)


def kernel(**inp):
    if "nc" not in _NC_CACHE:
        _NC_CACHE["nc"] = build_program()
    nc, in_names = _NC_CACHE["nc"]
    in_maps = make_in_maps(inp, in_names)
    res = run_bass_kernel_spmd(nc, in_maps, core_ids=list(range(NCORES)))
    return gather(res.results)
```
